# Optimizing a Trainium2 kernel written in Bass

```python
import jax, jax.numpy as jnp
from jax import lax
import numpy as np

D_MODEL = 1024
BATCH = 8
SEQ = 4096
DEPTH = 1

MIX_WIDTH = D_MODEL
POOL_WIDTH = MIX_WIDTH // 2
POOL_WINDOWS = (2, 4, 8, 16)
N_POOL_GROUPS = len(POOL_WINDOWS)
POOL_GROUP = POOL_WIDTH // N_POOL_GROUPS
NA_WIDTH = MIX_WIDTH - POOL_WIDTH
NA_HEAD_DIM = 64
NA_HEADS = NA_WIDTH // NA_HEAD_DIM
GRID_W = 64
WIN_ROWS = 8
WIN_COLS = 16
COL_BLOCK = 16
COL_BAND = 2 * WIN_COLS
N_COL_BLOCKS = GRID_W // COL_BLOCK
D_FF = 4 * D_MODEL
IN_WIDTH = POOL_WIDTH + 3 * NA_WIDTH
EPS = 1e-6

kernel_name = "hybrid_pool_neighbourhood_attn_block"


def rmsnorm(x, g):
    x32 = x.astype(jnp.float32)
    y = x32 * lax.rsqrt(jnp.mean(jnp.square(x32), axis=-1, keepdims=True) + EPS)
    return y.astype(x.dtype) * g


def pool_mixer(u, w_pool, pool_scale):
    B, T, _ = u.shape
    u32 = u.astype(jnp.float32)
    cs = jnp.concatenate([jnp.zeros((B, 1, POOL_WIDTH), jnp.float32),
                          jnp.cumsum(u32, axis=1)], axis=1)
    t = jnp.arange(T)
    outs = []
    for gi, w in enumerate(POOL_WINDOWS):
        sl = slice(gi * POOL_GROUP, (gi + 1) * POOL_GROUP)
        lo = jnp.clip(t - w // 2, 0, T)
        hi = jnp.clip(t - w // 2 + w, 0, T)
        cs_g = cs[:, :, sl]
        win_sum = jnp.take(cs_g, hi, axis=1) - jnp.take(cs_g, lo, axis=1)
        mean = win_sum / (hi - lo).astype(jnp.float32)[None, :, None]
        outs.append(mean - u32[:, :, sl])
    pooled = jnp.stack(outs, axis=2).astype(u.dtype)
    mixed = jnp.einsum('btgc,gcd->btgd', pooled, w_pool)
    return mixed.reshape(B, T, POOL_WIDTH) * pool_scale


def neighbourhood_attention(q, k, v, rpb):
    B, T, H, dh = q.shape
    rows = T // GRID_W
    kr = min(WIN_ROWS, rows)
    scale = dh ** -0.5

    def to_grid(a):
        return a.reshape(B, rows, GRID_W, H, dh).transpose(0, 3, 1, 2, 4)

    q_g, k_g, v_g = to_grid(q), to_grid(k), to_grid(v)

    cols = np.arange(GRID_W)
    c0 = np.clip(cols - WIN_COLS // 2, 0, GRID_W - WIN_COLS)
    band_start = np.clip(np.arange(N_COL_BLOCKS) * COL_BLOCK - WIN_COLS // 2, 0, GRID_W - COL_BAND)
    key_cols = band_start[:, None] + np.arange(COL_BAND)
    q_cols = (np.arange(N_COL_BLOCKS) * COL_BLOCK)[:, None] + np.arange(COL_BLOCK)
    qc0 = c0[q_cols][:, :, None]
    kc = key_cols[:, None, :]
    col_ok = (kc >= qc0) & (kc < qc0 + WIN_COLS)
    dc_idx = np.clip(kc - q_cols[:, :, None] + WIN_COLS - 1, 0, 2 * WIN_COLS - 2)
    bias_c = rpb[:, :, dc_idx]
    col_ok_b = jnp.asarray(col_ok)[:, :, None, :]

    def row_block(r):
        r0 = jnp.clip(r - kr // 2, 0, rows - kr)
        k_rows = lax.dynamic_slice_in_dim(k_g, r0, kr, axis=2)
        v_rows = lax.dynamic_slice_in_dim(v_g, r0, kr, axis=2)
        k_band = jnp.take(k_rows, key_cols, axis=3)
        v_band = jnp.take(v_rows, key_cols, axis=3)
        q_row = lax.dynamic_index_in_dim(q_g, r, axis=2, keepdims=False)
        q_row = q_row.reshape(B, H, N_COL_BLOCKS, COL_BLOCK, dh)
        s = jnp.einsum('bhjqd,bhijkd->bhjqik', q_row, k_band).astype(jnp.float32) * scale
        dr_idx = r0 + jnp.arange(kr) - r + (WIN_ROWS - 1)
        bias = jnp.take(bias_c, dr_idx, axis=1).transpose(0, 2, 3, 1, 4)
        s = s + bias[None].astype(jnp.float32)
        s = jnp.where(col_ok_b, s, -jnp.inf)
        p = jax.nn.softmax(s.reshape(B, H, N_COL_BLOCKS, COL_BLOCK, kr * COL_BAND), axis=-1)
        p = p.reshape(B, H, N_COL_BLOCKS, COL_BLOCK, kr, COL_BAND).astype(v.dtype)
        o = jnp.einsum('bhjqik,bhijkd->bhjqd', p, v_band)
        return o.reshape(B, H, GRID_W, dh)

    out = lax.map(row_block, jnp.arange(rows))
    return out.transpose(1, 0, 3, 2, 4).reshape(B, T, H * dh)


def setup_inputs(seed: int = 0) -> dict:
    key = jax.random.key(seed)
    ks = jax.random.split(key, 12)
    f32 = jnp.float32
    x = jax.random.normal(ks[0], (BATCH, SEQ, D_MODEL), f32)
    norm_mix_g = 1.0 + 0.02 * jax.random.normal(ks[1], (DEPTH, D_MODEL), f32)
    w_in = jax.random.normal(ks[2], (DEPTH, D_MODEL, IN_WIDTH), f32) * D_MODEL ** -0.5
    w_pool = jax.random.normal(ks[3], (DEPTH, N_POOL_GROUPS, POOL_GROUP, POOL_GROUP), f32) * POOL_GROUP ** -0.5
    pool_scale = 1.0 + 0.1 * jax.random.normal(ks[4], (DEPTH, POOL_WIDTH), f32)
    rpb = 0.5 * jax.random.normal(ks[5], (DEPTH, NA_HEADS, 2 * WIN_ROWS - 1, 2 * WIN_COLS - 1), f32)
    w_out = jax.random.normal(ks[6], (DEPTH, MIX_WIDTH, D_MODEL), f32) * MIX_WIDTH ** -0.5
    norm_mlp_g = 1.0 + 0.02 * jax.random.normal(ks[7], (DEPTH, D_MODEL), f32)
    w_up = jax.random.normal(ks[8], (DEPTH, D_MODEL, D_FF), f32) * D_MODEL ** -0.5
    w_down = jax.random.normal(ks[9], (DEPTH, D_FF, D_MODEL), f32) * D_FF ** -0.5
    final_g = 1.0 + 0.02 * jax.random.normal(ks[10], (D_MODEL,), f32)
    return {"x": x, "norm_mix_g": norm_mix_g, "w_in": w_in, "w_pool": w_pool,
            "pool_scale": pool_scale, "rpb": rpb, "w_out": w_out,
            "norm_mlp_g": norm_mlp_g, "w_up": w_up, "w_down": w_down,
            "final_g": final_g}


def reference(x, norm_mix_g, w_in, w_pool, pool_scale, rpb, w_out,
              norm_mlp_g, w_up, w_down, final_g):
    B, T, _ = x.shape
    for l in range(DEPTH):
        h = rmsnorm(x, norm_mix_g[l])
        proj = h @ w_in[l]
        u = proj[..., :POOL_WIDTH]
        q, k, v = jnp.split(proj[..., POOL_WIDTH:], 3, axis=-1)
        q = q.reshape(B, T, NA_HEADS, NA_HEAD_DIM)
        k = k.reshape(B, T, NA_HEADS, NA_HEAD_DIM)
        v = v.reshape(B, T, NA_HEADS, NA_HEAD_DIM)
        a = pool_mixer(u, w_pool[l], pool_scale[l])
        b = neighbourhood_attention(q, k, v, rpb[l])
        x = x + jnp.concatenate([a, b], axis=-1) @ w_out[l]
        h = rmsnorm(x, norm_mlp_g[l])
        x = x + jnp.square(jax.nn.relu(h @ w_up[l])) @ w_down[l]
    return rmsnorm(x, final_g)
```

```python
import numpy as np
from contextlib import ExitStack
import concourse.bass as bass
import concourse.mybir as mybir
from concourse.bass_utils import run_bass_kernel_spmd

F32 = mybir.dt.float32
BF16 = mybir.dt.bfloat16
AF = mybir.ActivationFunctionType
ALU = mybir.AluOpType

T = 4096
D = 1024
DFF = 4096
NT = 32
EPS = 1e-6
SB_BASE = 16512
SB_END = 229376
NEG = -200.0


class Buf:
    __slots__ = ("w", "r")

    def __init__(self):
        self.w = {}
        self.r = {}


class Prog:
    ENGS = ("pe", "act", "dve", "pool", "sp")

    def __init__(self, nc, stack):
        self.nc = nc
        self.stack = stack
        self.ops = {e: [] for e in self.ENGS}
        self.count = {e: 0 for e in self.ENGS}
        self.waited = {e: {} for e in self.ENGS}
        self.esem = {e: stack.enter_context(nc.semaphore("s_" + e)) for e in self.ENGS if e != "sp"}
        self.dsem = {}
        self.dcount = {}
        self.all_dma = {}

    def _collect(self, reads, writes, deps):
        d = {}

        def add(k, v):
            if d.get(k, 0) < v:
                d[k] = v
        for b in reads:
            for k, v in b.w.items():
                add(k, v)
        for b in writes:
            for k, v in b.w.items():
                add(k, v)
            for k, v in b.r.items():
                add(k, v)
        for t in deps:
            if t is not None:
                add(t[0], t[1])
        return d

    def _waits(self, eng, d):
        w = []
        for key, val in d.items():
            if eng == "pe" and key == "pe":
                continue
            if self.waited[eng].get(key, 0) >= val:
                continue
            self.waited[eng][key] = val
            w.append((key, val))
        return w

    def _note(self, tok, reads, writes):
        k, v = tok
        for b in reads:
            if b.r.get(k, 0) < v:
                b.r[k] = v
        for b in writes:
            b.w = {k: v}
            b.r = {}

    def op(self, eng, fn, reads=(), writes=(), deps=(), signal=True):
        w = self._waits(eng, self._collect(reads, writes, deps))
        if signal:
            self.count[eng] += 1
            tok = (eng, self.count[eng])
            self.ops[eng].append((fn, w, (eng, 1)))
        else:
            assert eng == "pe"
            tok = (eng, self.count[eng] + 1)
            self.ops[eng].append((fn, w, None))
        self._note(tok, reads, writes)
        return tok

    def dma(self, eng, fn, slot, reads=(), writes=(), deps=()):
        if slot not in self.dsem:
            self.dsem[slot] = self.stack.enter_context(self.nc.semaphore("d_" + slot))
            self.dcount[slot] = 0
        w = self._waits(eng, self._collect(reads, writes, deps))
        self.dcount[slot] += 16
        tok = ("D" + slot, self.dcount[slot])
        self.ops[eng].append((fn, w, ("D" + slot, 16)))
        self.all_dma["D" + slot] = self.dcount[slot]
        self._note(tok, reads, writes)
        return tok

    def barrier(self, dma=True):
        toks = [(e, self.count[e]) for e in self.ENGS if e != "sp" and self.count[e] > 0]
        if dma:
            toks += list(self.all_dma.items())
        return toks

    def _sem(self, key):
        if key.startswith("D"):
            return self.dsem[key[1:]]
        return self.esem[key]

    def replay(self, block, final_deps):
        def run(engname):
            def body(e):
                for fn, w, inc in self.ops[engname]:
                    for key, val in w:
                        e.wait_ge(self._sem(key), val)
                    ins = fn(e)
                    if inc is not None:
                        ins.then_inc(self._sem(inc[0]), inc[1])
                if engname == "sp":
                    for key, val in final_deps:
                        e.wait_ge(self._sem(key), val)
            return body

        block.tensor(run("pe"))
        block.scalar(run("act"))
        block.vector(run("dve"))
        block.gpsimd(run("pool"))
        block.sync(run("sp"))


class Alloc:
    def __init__(self, nc, prefix):
        self.nc = nc
        self.prefix = prefix
        self.off = SB_BASE

    def __call__(self, name, shape, dt):
        n = 1
        for s in shape[1:]:
            n *= s
        nbytes = n * (4 if dt == F32 else 2)
        off = (self.off + 31) // 32 * 32
        self.off = off + nbytes
        assert self.off <= SB_END, (name, self.off)
        return self.nc.alloc_sbuf_tensor_at(self.prefix + name, shape, dt, offset=off)


def pair_chunks(m):
    if m == 0:
        return [(0, 4), (1, 5), (2, 7), (3, 8)]
    if m == 1:
        return [(0, 3), (1, 4), (2, 5), (3, 7)]
    if m == 30:
        return [(28, 1), (29, 3), (30, 4), (31, 5)]
    if m == 31:
        return [(28, 0), (29, 1), (30, 3), (31, 4)]
    return [(m - 2, 2), (m - 1, 3), (m, 4), (m + 1, 5), (m + 2, 6)]


def build_nc(stage="full"):
    nc = bass.Bass("TRN2", target_bir_lowering=False)

    def din(name, shape):
        return nc.dram_tensor(name, shape, F32, kind="ExternalInput").ap()

    x = din("x", [T, D])
    g_mix = din("g_mix", [128, 8])
    g_mlp = din("g_mlp", [1, D])
    g_fin = din("g_fin", [1, D])
    w_in = din("w_in", [D, 2048])
    w_pool = din("w_pool", [4, 128, 128])
    pscale = din("pscale", [128, 4])
    invc = din("invc", [1, 64])
    btab = din("btab", [9, 128, 1024])
    w_out = din("w_out", [D, D])
    w_up = din("w_up", [D, DFF])
    w_down = din("w_down", [DFF, D])
    ident_d = din("ident", [128, 128])
    out = nc.dram_tensor("out", [T, D], F32, kind="ExternalOutput").ap()
    x1s = nc.dram_tensor("x1s", [T, D], F32).ap()
    wup_bf = nc.dram_tensor("wup_bf", [128, 8, DFF], BF16).ap()
    wdn_bf = nc.dram_tensor("wdn_bf", [128, 32, D], BF16).ap()

    with ExitStack() as st:
        P = Prog(nc, st)
        big = [nc.alloc_psum_tensor("big%d" % i, [128, 512], F32) for i in range(2)]
        Tb = nc.alloc_psum_tensor("Tb", [128, 1024], BF16)
        Sb = [[nc.alloc_psum_tensor("S%d%d" % (par, s_), [128, 2, 2, 128], F32) for s_ in range(2)] for par in range(2)]
        Vb = nc.alloc_psum_tensor("Vb", [128, 512], F32)
        b_big = [Buf(), Buf()]
        b_Tb = Buf()
        b_Sb = [[Buf(), Buf()], [Buf(), Buf()]]
        b_Vb = Buf()

        B = Alloc(nc, "b_")
        w_up_sb = B("w_up", [128, 8, DFF], BF16)
        c_wup = [Buf() for _ in range(8)]
        c_wups = [Buf() for _ in range(8)]
        c_wdns = [Buf() for _ in range(8)]
        A = Alloc(nc, "a_")
        NXA, NXR = 4, 4
        w_in_sb = A("w_in", [128, 8, 2048], BF16)
        hT = A("hT", [128, 8, 512], BF16)
        xa = A("xa", [128, NXA, D], F32)
        hb = A("hb", [128, 4, D], BF16)
        assert A.off == SB_BASE + 65536
        ident = A("ident", [128, 128], BF16)
        w_out_sb = A("w_out", [128, 8, 1024], BF16)
        w_pool_sb = A("w_pool", [128, 4, 128], BF16)
        pscale_sb = A("pscale", [128, 4], F32)
        gmix_sb = A("gmix", [128, 8], F32)
        invc_sb = A("invc", [128, 2, 4, 8], F32)
        eps_sb = A("eps", [128, 1], F32)
        Ttab = A("Ttab", [128, 9, 8, 128], BF16)
        qT = A("qT", [128, 2, 4, 512], BF16)
        kT = A("kT", [128, 3, 4, 512], BF16)
        Va = A("Va", [128, 3, 4, 8, 65], BF16)
        xr = A("xr", [128, NXR, D], F32)
        ss = A("ss", [128, 4], F32)
        lnv = A("lnv", [128, 4], F32)
        rstd = A("rstd", [128, 4], F32)
        Pb = A("Pb", [128, 2, 6, 4, 128], BF16)
        catT = A("catT", [128, 8, 512], BF16)
        bt = A("bt", [128, 2, 512], BF16)
        rden = A("rden", [128, 2, 4], F32)
        tail_start = A.off
        U = A("U", [128, 2, 4, 528], F32)
        invw = A("invw", [128, 4, 512], F32)
        pooled = A("pooled", [128, 4, 512], BF16)
        TA = A("TA", [128, 528], F32)
        TBt = A("TBt", [128, 528], F32)
        tmp8 = A("tmp8", [128, 8], F32)
        b_invw = Buf()

        b_ident, b_wpool, b_ps, b_gmix, b_invc, b_eps = Buf(), Buf(), Buf(), Buf(), Buf(), Buf()
        b_win = [Buf() for _ in range(4)]
        b_wout = [Buf() for _ in range(8)]
        b_Ttab = [Buf() for _ in range(9)]
        b_stage = [Buf(), Buf()]
        b_q = [[Buf() for _ in range(4)] for _ in range(2)]
        b_k = [[Buf() for _ in range(4)] for _ in range(3)]
        b_v = [[Buf() for _ in range(4)] for _ in range(3)]
        b_U = [[Buf() for _ in range(4)] for _ in range(2)]
        b_hT = [Buf() for _ in range(4)]
        b_xa = [Buf() for _ in range(NXA)]
        b_xr = [Buf() for _ in range(NXR)]
        b_hb = [Buf() for _ in range(4)]
        b_TA, b_TB, b_tmp8 = Buf(), Buf(), Buf()
        b_ss = [Buf() for _ in range(4)]
        b_Pb = [[[Buf(), Buf()] for _ in range(6)] for _ in range(2)]
        b_pooled = [Buf() for _ in range(4)]
        b_cat = [Buf() for _ in range(4)]
        b_catb = [Buf() for _ in range(4)]
        b_bt = [Buf(), Buf()]
        b_rden = [Buf(), Buf()]
        b_x1s = [Buf() for _ in range(NT)]

        P.dma("pool", lambda e: e.dma_start(out=ident[:], in_=ident_d), "ident", writes=[b_ident])
        P.op("pool", lambda e: e.memset(eps_sb[:], EPS), writes=[b_eps])
        for g in range(4):
            P.op("pool", lambda e, g=g: e.memset(invw[:, g, :], 1.0 / (2 << g)), writes=[b_invw])
        P.op("pool", lambda e: e.memset(Va[:, :, :, :, 64:65], 1.0), writes=[b for r in b_v for b in r])
        P.op("pool", lambda e: e.memset(U[:, 0, :, 0:8], 0.0), writes=b_U[0])
        P.dma("sp", lambda e: e.dma_start(out=gmix_sb[:], in_=g_mix), "gmix", writes=[b_gmix])
        P.dma("sp", lambda e: e.dma_start(out=pscale_sb[:], in_=pscale), "pscale", writes=[b_ps])
        P.dma("sp", lambda e: e.dma_start(out=invc_sb[:].rearrange("p a g t -> p (a g t)"), in_=invc[0, :].partition_broadcast(128)), "invc", writes=[b_invc])
        for i in range(4):
            P.dma("sp", lambda e, i=i: e.dma_start(out=xa[:, i, :], in_=x[i * 128:(i + 1) * 128, :]), "xa%d" % i, writes=[b_xa[i]])
        xr_stage = xr[:].rearrange("p s (c n) -> p (s c) n", c=2)

        def load_win_sw(blk):
            P.dma("pool", lambda e: e.dma_start(out=w_in_sb[:, :, blk * 512:(blk + 1) * 512],
                                                in_=w_in[:, blk * 512:(blk + 1) * 512].rearrange("(c p) n -> p c n", p=128)),
                  "win%d" % blk, writes=[b_win[blk]])

            def fold():
                for kc in range(8):
                    P.op("dve", lambda e, kc=kc: e.tensor_scalar(out=w_in_sb[:, kc, blk * 512:(blk + 1) * 512], in0=w_in_sb[:, kc, blk * 512:(blk + 1) * 512],
                                                                 scalar1=gmix_sb[:, kc:kc + 1], scalar2=None, op0=ALU.mult),
                         reads=[b_gmix], writes=[b_win[blk]])
            return fold

        def load_win_hw(blk):
            P.dma("sp", lambda e: e.dma_start(out=xr_stage, in_=w_in[:, blk * 512:(blk + 1) * 512].rearrange("(c p) n -> p c n", p=128)),
                  "stage", writes=b_xr)

            def fold():
                for kc in range(8):
                    P.op("dve", lambda e, kc=kc: e.tensor_scalar(out=w_in_sb[:, kc, blk * 512:(blk + 1) * 512], in0=xr_stage[:, kc, :],
                                                                 scalar1=gmix_sb[:, kc:kc + 1], scalar2=None, op0=ALU.mult),
                         reads=[b_gmix] + b_xr, writes=[b_win[blk]])
            return fold
        fold0 = load_win_hw(0)
        fold2 = load_win_sw(2)
        fold1 = load_win_sw(1)
        P.dma("pool", lambda e: e.dma_start(out=w_pool_sb[:], in_=w_pool.rearrange("g c d -> c g d")), "wpool", writes=[b_wpool])
        state = {"nb": 0, "sset": 0, "pb": 0, "ev": 0}

        def next_big():
            i = state["nb"] % 2
            state["nb"] += 1
            return big[i], b_big[i]

        def copy_evac(out_ap, in_ap, reads, writes, eng=None, scale=None):
            if eng is None:
                eng = "dve"
            if eng == "act":
                if scale is not None:
                    return P.op("act", lambda e: e.activation(out=out_ap, in_=in_ap, func=AF.Copy, scale=scale), reads=reads, writes=writes)
                return P.op("act", lambda e: e.copy(out=out_ap, in_=in_ap), reads=reads, writes=writes)
            if scale is not None:
                return P.op("dve", lambda e: e.tensor_scalar(out=out_ap, in0=in_ap, scalar1=scale, scalar2=None, op0=ALU.mult), reads=reads, writes=writes)
            return P.op("dve", lambda e: e.tensor_copy(out=out_ap, in_=in_ap), reads=reads, writes=writes)

        def rms_h(xt_ap, b_x, hslot, col):
            P.op("act", lambda e: e.activation(out=hb[:, hslot, :], in_=xt_ap, func=AF.Square, accum_out=ss[:, col:col + 1]),
                 reads=[b_x], writes=[b_hb[hslot], b_ss[col]])
            P.op("act", lambda e: e.activation(out=lnv[:, col:col + 1], in_=ss[:, col:col + 1], func=AF.Ln, scale=1.0 / D, bias=eps_sb[:, 0:1]),
                 reads=[b_eps], writes=[b_ss[col]])
            P.op("act", lambda e: e.activation(out=rstd[:, col:col + 1], in_=lnv[:, col:col + 1], func=AF.Exp, scale=-0.5),
                 writes=[b_ss[col]])

            def dve_part():
                P.op("dve", lambda e: e.tensor_scalar(out=hb[:, hslot, :], in0=xt_ap, scalar1=rstd[:, col:col + 1], scalar2=None, op0=ALU.mult),
                     reads=[b_x, b_ss[col]], writes=[b_hb[hslot]])
            return dve_part

        def load_xa(s):
            for i in range(4):
                tile = 4 * s + i
                sl = tile % NXA
                P.dma("sp", lambda e, tile=tile, sl=sl: e.dma_start(out=xa[:, sl, :], in_=x[tile * 128:(tile + 1) * 128, :]), "xa%d" % sl, writes=[b_xa[sl]])

        def load_xr(s):
            for i in range(4):
                tile = 4 * s + i
                sl = tile % NXR
                P.dma("sp", lambda e, tile=tile, sl=sl: e.dma_start(out=xr[:, sl, :], in_=x[tile * 128:(tile + 1) * 128, :]), "xr%d" % sl, writes=[b_xr[sl]])

        def prep_norm(s, tiles=(0, 1, 2, 3)):
            dparts = []
            for i in tiles:
                tile = 4 * s + i
                sl = tile % NXA
                dparts.append(rms_h(xa[:, sl, :], b_xa[sl], i, i))
            return dparts

        def prep_T_tile(i):
            for kc in range(8):
                P.op("pe", lambda e, kc=kc, i=i: e.transpose(out=Tb[:, kc * 128:(kc + 1) * 128], in_=hb[:, i, kc * 128:(kc + 1) * 128], identity=ident[:]),
                     reads=[b_hb[i], b_ident], writes=[b_Tb], signal=(kc == 7))
            copy_evac(hT[:, :, i * 128:(i + 1) * 128], Tb[:].rearrange("p (k n) -> p k n", k=8), [b_Tb], [b_hT[i]])

        def prep_T_halo(s):
            su = s % 2
            if s >= 1:
                P.op("pool", lambda e: e.tensor_copy(out=U[:, su, :, 0:8], in_=U[:, 1 - su, :, 512:520]), reads=b_U[1 - su], writes=b_U[su])

        def prep_T(s):
            for i in range(4):
                prep_T_tile(i)
            prep_T_halo(s)

        def proj_groups(s):
            sq, sk, su = s % 2, s % 3, s % 2

            def fm_group(oc):
                def run():
                    bank, bb = next_big()
                    for kc in range(8):
                        P.op("pe", lambda e, kc=kc, bank=bank: e.matmul(bank[:], lhsT=w_in_sb[:, kc, oc * 128:(oc + 1) * 128], rhs=hT[:, kc, :],
                                                                        start=(kc == 0), stop=(kc == 7)),
                             reads=[b_win[oc // 4]] + b_hT, writes=[bb], signal=(kc == 7))
                    j = oc % 4
                    if oc < 4:
                        copy_evac(U[:, su, j, 8:520], bank[:], [bb], [b_U[su][j]])
                    elif oc < 8:
                        copy_evac(qT[:, sq, j, :], bank[:], [bb], [b_q[sq][j]], scale=0.125)
                    else:
                        copy_evac(kT[:, sk, j, :], bank[:], [bb], [b_k[sk][j]])
                return run

            def v_group(i):
                def run():
                    bank, bb = next_big()
                    for kc in range(8):
                        P.op("pe", lambda e, kc=kc, bank=bank: e.matmul(bank[:], lhsT=hT[:, kc, i * 128:(i + 1) * 128], rhs=w_in_sb[:, kc, 1536:2048],
                                                                        start=(kc == 0), stop=(kc == 7)),
                             reads=[b_win[3], b_hT[i]], writes=[bb], signal=(kc == 7))
                    copy_evac(Va[:, sk, i, :, 0:64], bank[:].rearrange("p (h d) -> p h d", h=8), [bb], [b_v[sk][i]])
                return run
            return [fm_group(oc) for oc in range(4)] + [fm_group(oc) for oc in range(8, 12)] + [v_group(i) for i in range(4)] + [fm_group(oc) for oc in range(4, 8)]

        def proj_halo(s):
            su = s % 2
            if s >= 1:
                P.op("pool", lambda e: e.tensor_copy(out=U[:, 1 - su, :, 520:528], in_=U[:, su, :, 8:16]), reads=b_U[su], writes=b_U[1 - su])
            if s == 7:
                P.op("pool", lambda e: e.memset(U[:, su, :, 520:528], 0.0), writes=b_U[su])

        def unit_fills(m, Q, pb):
            chunks = pair_chunks(m)
            sq = (m // 4) % 2
            qc0 = (m % 4) * 128
            ncp = (len(chunks) + 1) // 2

            def mk(cp):
                def run():
                    sset = state["sset"] % 2
                    state["sset"] += 1
                    cis = [ci for ci in (2 * cp, 2 * cp + 1) if ci < len(chunks)]
                    n = len(cis)
                    tids = [chunks[ci][1] for ci in cis]
                    tstep = (tids[1] - tids[0]) if n == 2 else 1
                    for par in range(2):
                        tab_ap = Ttab[:, tids[0]:tids[-1] + 1:tstep, 4 * Q + par:4 * Q + par + 3:2, :]
                        P.op("pe", lambda e, par=par, tab_ap=tab_ap: e.matmul(Sb[par][sset][:, 0:n, :, :], lhsT=ident[:], rhs=tab_ap, start=True, stop=False,
                                                                             skip_group_check=True),
                             reads=[b_ident] + [b_Ttab[t_] for t_ in tids], writes=[b_Sb[par][sset]], signal=False)
                    for ci in cis:
                        c, tid = chunks[ci]
                        sk = (c // 4) % 3
                        kc0 = (c % 4) * 128
                        for h4 in range(4):
                            h = 4 * Q + h4
                            j, par = h // 2, h % 2
                            last = (ci == cis[-1] and h4 >= 2)
                            P.op("pe", lambda e, par=par, ci=ci, h4=h4, sk=sk, j=j, kc0=kc0:
                                 e.matmul(Sb[par][sset][:, ci % 2, h4 // 2, :],
                                          lhsT=kT[par * 64:(par + 1) * 64, sk, j, kc0:kc0 + 128],
                                          rhs=qT[par * 64:(par + 1) * 64, sq, j, qc0:qc0 + 128], start=False, stop=True, skip_group_check=True),
                                 reads=[b_k[sk][j], b_q[sq][j]], writes=[b_Sb[par][sset]], signal=last)
                    for par in range(2):
                        P.op("act", lambda e, par=par:
                             e.activation(out=Pb[:, pb, 2 * cp:2 * cp + n, par:4:2, :], in_=Sb[par][sset][:, 0:n, :, :], func=AF.Exp),
                             reads=[b_Sb[par][sset]], writes=[b_Pb[pb][ci_][par] for ci_ in cis])
                return run
            return [mk(cp) for cp in range(ncp)]

        def attn_pv(m, Q, pb):
            chunks = pair_chunks(m)
            bsl = m % 2
            Vv = Vb[:, 0:260].rearrange("p (h d) -> p h d", h=4)
            for h4 in range(4):
                for ci, (c, tid) in enumerate(chunks):
                    sk = (c // 4) % 3
                    P.op("pe", lambda e, h4=h4, ci=ci, c=c, sk=sk, pb=pb, Q=Q:
                         e.matmul(Vb[:, h4 * 65:(h4 + 1) * 65], lhsT=Pb[:, pb, ci, h4, :], rhs=Va[:, sk, c % 4, 4 * Q + h4, :],
                                  start=(ci == 0), stop=(ci == len(chunks) - 1)),
                         reads=[b_Pb[pb][ci][h4 % 2], b_v[sk][c % 4]], writes=[b_Vb], signal=(h4 == 3 and ci == len(chunks) - 1))
            P.op("dve", lambda e: e.reciprocal(out=rden[:, bsl, :], in_=Vv[:, :, 64]), reads=[b_Vb], writes=[b_rden[bsl]])
            P.op("dve", lambda e: e.tensor_tensor(out=bt[:, bsl, Q * 256:(Q + 1) * 256].rearrange("p (h d) -> p h d", h=4), in0=Vv[:, :, 0:64],
                                                  in1=rden[:, bsl, :].unsqueeze(2).broadcast_to([128, 4, 64]), op=ALU.mult),
                 reads=[b_Vb, b_rden[bsl]], writes=[b_bt[bsl]])

        def attn_finish(m):
            bsl = m % 2
            i = m % 4
            for j in range(4):
                P.op("pe", lambda e, j=j: e.transpose(out=Tb[:, j * 128:(j + 1) * 128], in_=bt[:, bsl, j * 128:(j + 1) * 128], identity=ident[:]),
                     reads=[b_bt[bsl], b_ident], writes=[b_Tb], signal=(j == 3))
            copy_evac(catT[:, 4:8, i * 128:(i + 1) * 128], Tb[:, 0:512].rearrange("p (k n) -> p k n", k=4), [b_Tb], [b_catb[i]], eng="act")

        def fin_pool(s):
            su = s % 2
            for g in range(4):
                w = 2 << g
                Ug = U[:, su, g, :]
                P.op("pool", lambda e, Ug=Ug: e.tensor_tensor(out=TA[:, 1:528], in0=Ug[:, 0:527], in1=Ug[:, 1:528], op=ALU.add), reads=[b_U[su][g]], writes=[b_TA])
                if g >= 1:
                    P.op("pool", lambda e: e.tensor_tensor(out=TBt[:, 2:527], in0=TA[:, 1:526], in1=TA[:, 3:528], op=ALU.add), reads=[b_TA], writes=[b_TB])
                if g >= 2:
                    P.op("pool", lambda e: e.tensor_tensor(out=TA[:, 4:525], in0=TBt[:, 2:523], in1=TBt[:, 6:527], op=ALU.add), reads=[b_TB], writes=[b_TA])
                if g >= 3:
                    P.op("pool", lambda e: e.tensor_tensor(out=TBt[:, 8:521], in0=TA[:, 4:517], in1=TA[:, 12:525], op=ALU.add), reads=[b_TA], writes=[b_TB])
                L, bL = (TA, b_TA) if g in (0, 2) else (TBt, b_TB)
                P.op("pool", lambda e, L=L, g=g: e.tensor_tensor(out=L[:, 8:520], in0=L[:, 8:520], in1=invw[:, g, :], op=ALU.mult),
                     reads=[b_invw], writes=[bL])
                P.op("pool", lambda e, L=L, Ug=Ug, g=g: e.tensor_tensor(out=pooled[:, g, :], in0=L[:, 8:520], in1=Ug[:, 8:520], op=ALU.subtract),
                     reads=[bL, b_U[su][g]], writes=[b_pooled[g]])
                if s == 0 or s == 7:
                    a = 0 if s == 0 else 1
                    lo = 8 if s == 0 else 512
                    P.op("pool", lambda e, L=L, a=a, lo=lo, g=g: e.tensor_tensor(out=tmp8[:], in0=L[:, lo:lo + 8], in1=invc_sb[:, a, g, :], op=ALU.mult),
                         reads=[bL, b_invc], writes=[b_tmp8])
                    P.op("pool", lambda e, Ug=Ug, lo=lo, g=g: e.tensor_tensor(out=pooled[:, g, lo - 8:lo], in0=tmp8[:], in1=Ug[:, lo:lo + 8], op=ALU.subtract),
                         reads=[b_tmp8, b_U[su][g]], writes=[b_pooled[g]])

        def fin_poolmm(s, gs=(0, 1, 2, 3)):
            for g in gs:
                bank, bb = next_big()
                P.op("pe", lambda e, g=g, bank=bank: e.matmul(bank[:], lhsT=w_pool_sb[:, g, :], rhs=pooled[:, g, :], start=True, stop=True),
                     reads=[b_wpool, b_pooled[g]], writes=[bb])
                P.op("act", lambda e, g=g, bank=bank: e.activation(out=catT[:, g, :], in_=bank[:], func=AF.Copy, scale=pscale_sb[:, g:g + 1]),
                     reads=[bb, b_ps], writes=[b_cat[g]])

        def fin_wout_half(s, i, half):
            tile = 4 * s + i
            sl = tile % NXR
            bank, bb = next_big()
            for c in range(8):
                P.op("pe", lambda e, c=c, bank=bank: e.matmul(bank[:], lhsT=catT[:, c, i * 128:(i + 1) * 128],
                                                              rhs=w_out_sb[:, c, half * 512:(half + 1) * 512], start=(c == 0), stop=(c == 7)),
                     reads=[b_wout[c], b_cat[c % 4] if c < 4 else b_catb[i]], writes=[bb], signal=(c == 7))
            P.op("dve", lambda e, bank=bank: e.tensor_tensor(out=xr[:, sl, half * 512:(half + 1) * 512], in0=xr[:, sl, half * 512:(half + 1) * 512],
                                                             in1=bank[:], op=ALU.add),
                 reads=[bb], writes=[b_xr[sl]])
            if half == 1:
                dst = out if stage == "A" else x1s
                P.dma("sp", lambda e: e.dma_start(out=dst[tile * 128:(tile + 1) * 128, :], in_=xr[:, sl, :]), "xr%d" % sl,
                      reads=[b_xr[sl]], writes=[b_x1s[tile]])

        def fin_wout(s, i):
            fin_wout_half(s, i, 0)
            fin_wout_half(s, i, 1)

        def late_setup():
            for t in range(9):
                sl = t % NXR
                P.dma("sp", lambda e, t=t, sl=sl: e.dma_start(out=xr[:, sl, :], in_=btab[t]), "stg%d" % sl, writes=[b_xr[sl]])
                if t % 2 == 0:
                    P.op("act", lambda e, t=t, sl=sl: e.copy(out=Ttab[:, t, :, :], in_=xr[:, sl, :].rearrange("p (h q) -> p h q", h=8)),
                         reads=[b_xr[sl]], writes=[b_Ttab[t]])
                else:
                    P.op("dve", lambda e, t=t, sl=sl: e.tensor_copy(out=Ttab[:, t, :, :], in_=xr[:, sl, :].rearrange("p (h q) -> p h q", h=8)),
                         reads=[b_xr[sl]], writes=[b_Ttab[t]])
            for c in range(8):
                P.dma("pool", lambda e, c=c: e.dma_start(out=w_out_sb[:, c, :], in_=w_out[c * 128:(c + 1) * 128, :]), "wout%d" % c, writes=[b_wout[c]])

        def precast(lo, hi):
            for c in range(lo, hi):
                if c < 8:
                    P.dma("pool", lambda e, c=c: e.dma_start(out=wup_bf[:, c, :], in_=w_up[c * 128:(c + 1) * 128, :]), "pcu%d" % c, writes=[c_wups[c]])
                else:
                    c2 = c - 8
                    P.dma("pool", lambda e, c2=c2: e.dma_start(out=wdn_bf[:, 4 * c2:4 * c2 + 4, :],
                                                               in_=w_down[c2 * 512:(c2 + 1) * 512, :].rearrange("(j p) d -> p j d", p=128)),
                          "pcd%d" % c2, writes=[c_wdns[c2]])

        def step(s):
            do_proj = s < 8
            do_fin = s >= 1
            if do_fin:
                load_xr(s - 1)
            filler = []
            if do_proj:
                if s + 1 < 8:
                    load_xa(s + 1)
                groups = proj_groups(s)
                for gfn in groups[:4]:
                    gfn()
                filler = groups[4:]
                proj_halo(s)
            if not do_fin:
                fold3 = load_win_hw(3)
                for gfn in filler[0:4]:
                    gfn()
                fold3()
                for gfn in filler[4:8]:
                    gfn()
                fold1()
                for gfn in filler[8:]:
                    gfn()
                for dp_ in prep_norm(s + 1):
                    dp_()
                prep_T(s + 1)
                return
            fin_pool(s - 1)
            if 2 <= s <= 5:
                precast(4 * (s - 2), 4 * (s - 1))
            popped = [0]
            nfill = [0]

            def fill(n=1):
                for _ in range(n):
                    if filler:
                        filler.pop(0)()
                        popped[0] += 1
            units = [(4 * (s - 1) + i, Q) for i in range(4) for Q in range(2)]
            pend_pv = None
            pend_fin = None
            pend_fin2 = [None]
            for k, (m, Q) in enumerate(units):
                pb = state["pb"] % 2
                state["pb"] += 1
                if k == 4 and do_proj:
                    while popped[0] < 8:
                        fill(1)
                fills = unit_fills(m, Q, pb)
                for f, ff in enumerate(fills):
                    ff()
                    if nfill[0] < 8 or nfill[0] % 2 == 0:
                        fill(1)
                    nfill[0] += 1
                    if f == 1 and pend_pv is not None:
                        attn_pv(*pend_pv)
                        if pend_pv[1] == 1:
                            pend_fin = pend_pv[0]
                        pend_pv = None
                    if f == 0 and pend_fin2[0] is not None:
                        attn_finish(pend_fin2[0])
                        filler.append(lambda i=pend_fin2[0] % 4: fin_wout_half(s - 1, i, 0))
                        filler.append(lambda i=pend_fin2[0] % 4: fin_wout_half(s - 1, i, 1))
                        pend_fin2[0] = None
                    if f == len(fills) - 1 and pend_fin is not None:
                        pend_fin2[0] = pend_fin
                        pend_fin = None
                if k == 3 and s != 8:
                    for g_ in (3, 2, 1, 0):
                        filler.insert(0, lambda g_=g_: fin_poolmm(s - 1, (g_,)))
                if k == 2 and s == 8:
                    fin_poolmm(s - 1)
                    if s == 8 and stage != "A":
                        phaseb_early_start()
                    if s == 8:
                        for c in range(8):
                            P.dma("sp", lambda e, c=c: e.dma_start(out=w_up_sb[:, c, :], in_=wup_bf[:, c, :]), "wup%d" % c, reads=[c_wups[c]], writes=[c_wup[c]], deps=state["bar7"])
                if s == 8 and stage != "A":
                    if k == 5:
                        state["dparts2"] = prep_norm2(0)
                    if k == 7:
                        for dp_ in state["dparts2"]:
                            dp_()
                if 1 <= k <= 4 and s + 1 < 8:
                    state["dp%d" % (k - 1)] = prep_norm(s + 1, (k - 1,))
                if 3 <= k <= 6 and s + 1 < 8:
                    for dp_ in state["dp%d" % (k - 3)]:
                        dp_()
                pend_pv = (m, Q, pb)
            if pend_fin2[0] is not None:
                attn_finish(pend_fin2[0])
                filler.append(lambda i=pend_fin2[0] % 4: fin_wout_half(s - 1, i, 0))
                filler.append(lambda i=pend_fin2[0] % 4: fin_wout_half(s - 1, i, 1))
                pend_fin2[0] = None
            fill(len(filler))
            nxt = s + 1 < 8
            last8 = (s == 8 and stage != "A")
            if nxt:
                prep_T_tile(0)
            if last8:
                prep_T2(0, 0)
            attn_pv(*pend_pv)
            if nxt:
                prep_T_tile(1)
                prep_T_tile(2)
            if last8:
                prep_T2(0, 1)
            attn_finish(pend_pv[0])
            if nxt:
                prep_T_tile(3)
                prep_T_halo(s + 1)
            fin_wout(s - 1, pend_pv[0] % 4)

        bar = {"cur": None}
        NXB = 6
        identb = B("ident", [128, 128], BF16)
        w_dn_sb = B("w_dn", [128, 32, D], BF16)
        upT = B("upT", [128, 32, 256], BF16)
        xbs = [None, None] + [B("xb%d" % i, [128, D], F32) for i in range(2, NXB)]
        rt = B("rt", [128, 3, 256], F32)
        hT2p = [None, B("hT2_1", [128, 8, 256], BF16)]
        gfin_sb = B("gfin", [128, D], F32)
        seq_end = B.off

        class TopAlloc:
            def __init__(self):
                self.off = SB_END - 256

            def __call__(self, name, shape, dt):
                n = 1
                for d_ in shape[1:]:
                    n *= d_
                nbytes = n * (4 if dt == F32 else 2)
                self.off = (self.off - nbytes) // 32 * 32
                return nc.alloc_sbuf_tensor_at("b_" + name, shape, dt, offset=self.off)
        Tp = TopAlloc()
        xbs[0] = Tp("xb0", [128, D], F32)
        xbs[1] = Tp("xb1", [128, D], F32)
        xs2 = Tp("xs2", [128, 2, D], F32)
        hb2 = Tp("hb2", [128, 2, D], BF16)
        hT2p[0] = Tp("hT2_0", [128, 8, 256], BF16)
        gmlp_sb = Tp("gmlp", [128, D], F32)
        jb = Tp("jb", [128, D], BF16)
        epsb = Tp("eps", [128, 1], F32)
        ss2 = Tp("ss2", [128, 4], F32)
        lnv2 = Tp("lnv2", [128, 4], F32)
        rstd2 = Tp("rstd2", [128, 4], F32)
        assert Tp.off >= tail_start and Tp.off >= seq_end, (Tp.off, tail_start, seq_end)
        print("phase B seq end", seq_end, "early start", Tp.off, "tail_start", tail_start)

        c_ident, c_gmlp, c_gfin, c_eps = b_ident, Buf(), Buf(), Buf()
        c_xs2 = [Buf(), Buf()]
        c_jb = Buf()
        c_wdn = [Buf() for _ in range(8)]
        c_upT = [Buf() for _ in range(32)]
        c_hT2 = [[Buf(), Buf()], [Buf(), Buf()]]
        c_xb = [Buf() for _ in range(NXB)]
        c_hb2 = [Buf(), Buf()]
        c_ss2 = [Buf() for _ in range(4)]
        c_rt = [Buf() for _ in range(3)]
        upbank = [Sb[0][0], Sb[0][1], Sb[1][0]]
        c_upbank = [b_Sb[0][0], b_Sb[0][1], b_Sb[1][0]]

        assert nc.lookup_mloc(identb).addr == nc.lookup_mloc(ident).addr
        identb = ident

        def rms_parts(xt_ap, b_x, g_sb, b_g, hslot, col, inplace=False):
            def act_part():
                jout, jbuf = (jb[:], c_jb) if inplace else (hb2[:, hslot, :], c_hb2[hslot])
                P.op("act", lambda e: e.activation(out=jout, in_=xt_ap, func=AF.Square, accum_out=ss2[:, col:col + 1]),
                     reads=[b_x], writes=[jbuf, c_ss2[col]], deps=bar["cur"])
                P.op("act", lambda e: e.activation(out=lnv2[:, col:col + 1], in_=ss2[:, col:col + 1], func=AF.Ln, scale=1.0 / D, bias=epsb[:, 0:1]),
                     reads=[c_eps], writes=[c_ss2[col]])
                P.op("act", lambda e: e.activation(out=rstd2[:, col:col + 1], in_=lnv2[:, col:col + 1], func=AF.Exp, scale=-0.5),
                     writes=[c_ss2[col]])
                if not inplace:
                    P.op("act", lambda e: e.activation(out=xs2[:, hslot, :], in_=xt_ap, func=AF.Copy, scale=rstd2[:, col:col + 1]),
                         reads=[b_x, c_ss2[col]], writes=[c_xs2[hslot]])
                else:
                    P.op("act", lambda e: e.activation(out=xt_ap, in_=xt_ap, func=AF.Copy, scale=rstd2[:, col:col + 1]),
                         reads=[c_ss2[col]], writes=[b_x])

            def dve_part():
                if not inplace:
                    P.op("dve", lambda e: e.tensor_tensor(out=hb2[:, hslot, :], in0=xs2[:, hslot, :], in1=g_sb[:], op=ALU.mult),
                         reads=[c_xs2[hslot], b_g], writes=[c_hb2[hslot]], deps=bar["cur"])
                else:
                    P.op("dve", lambda e: e.tensor_tensor(out=xt_ap, in0=xt_ap, in1=g_sb[:], op=ALU.mult),
                         reads=[b_g], writes=[b_x])
            return act_part, dve_part

        def load_xb(t):
            for i in range(2):
                tile = 2 * t + i
                sl = tile % NXB
                P.dma("sp", lambda e, tile=tile, sl=sl: e.dma_start(out=xbs[sl][:], in_=x1s[tile * 128:(tile + 1) * 128, :]), "xb%d" % sl,
                      reads=[b_x1s[tile]], writes=[c_xb[sl]], deps=bar["cur"])

        NSTEP = 16
        pending = []

        def flush_pending():
            while pending:
                pending.pop(0)()

        def prep_norm2(t):
            dparts = []
            for i in range(2):
                tile = 2 * t + i
                sl = tile % NXB
                ap_, dp_ = rms_parts(xbs[sl][:], c_xb[sl], gmlp_sb, c_gmlp, i, i)
                ap_()
                dparts.append(dp_)
            return dparts

        def prep_T2(t, i):
            hs = t % 2
            for kc in range(8):
                P.op("pe", lambda e, kc=kc, i=i: e.transpose(out=Tb[:, kc * 128:(kc + 1) * 128], in_=hb2[:, i, kc * 128:(kc + 1) * 128], identity=identb[:]),
                     reads=[c_hb2[i], c_ident], writes=[b_Tb], signal=(kc == 7), deps=bar["cur"])
            P.op("dve", lambda e: e.tensor_copy(out=hT2p[hs][:, :, i * 128:(i + 1) * 128], in_=Tb[:].rearrange("p (k n) -> p k n", k=8)),
                 reads=[b_Tb], writes=[c_hT2[hs][i]], deps=bar["cur"])

        def up(t):
            hs = t % 2
            for j in range(32):
                if j == 4:
                    flush_pending()
                if j == 5 and t + 2 < NSTEP:
                    load_xb(t + 2)
                if t + 1 < NSTEP:
                    if j == 6:
                        state["dparts"] = prep_norm2(t + 1)
                    if j == 14:
                        for dp_ in state["dparts"]:
                            dp_()
                    if j == 18:
                        prep_T2(t + 1, 0)
                    if j == 25:
                        prep_T2(t + 1, 1)
                bk, bbk = upbank[j % 3], c_upbank[j % 3]
                bkv = bk[:].rearrange("p a b n -> p (a b n)")[:, 0:256]
                for kc in range(8):
                    P.op("pe", lambda e, kc=kc, j=j, bkv=bkv: e.matmul(bkv, lhsT=w_up_sb[:, kc, j * 128:(j + 1) * 128], rhs=hT2p[hs][:, kc, :], start=(kc == 0), stop=(kc == 7)),
                         reads=[c_wup[kc]] + c_hT2[hs], writes=[bbk], signal=(kc == 7))
                r = j % 3
                if j % 4 == 1 and not (2 <= j <= 10):
                    P.op("act", lambda e, r=r, bkv=bkv: e.activation(out=rt[:, r, :], in_=bkv, func=AF.Relu), reads=[bbk], writes=[c_rt[r]], deps=bar["cur"])
                else:
                    P.op("dve", lambda e, r=r, bkv=bkv: e.tensor_scalar(out=rt[:, r, :], in0=bkv, scalar1=0.0, scalar2=None, op0=ALU.max), reads=[bbk], writes=[c_rt[r]], deps=bar["cur"])
                P.op("pool", lambda e, r=r, j=j: e.tensor_tensor(out=upT[:, j, :], in0=rt[:, r, :], in1=rt[:, r, :], op=ALU.mult), reads=[c_rt[r]], writes=[c_upT[j]], deps=bar["cur"])

        def down(t):
            for i in range(2):
                tile = 2 * t + i
                sl = tile % NXB
                for half in range(2):
                    bank, bb = next_big()
                    if i == 1 and half == 1:
                        flush_pending()
                    for j in range(32):
                        P.op("pe", lambda e, j=j, i=i, half=half, bank=bank: e.matmul(bank[:], lhsT=upT[:, j, i * 128:(i + 1) * 128],
                                                                                       rhs=w_dn_sb[:, j, half * 512:(half + 1) * 512], start=(j == 0), stop=(j == 31)),
                             reads=[c_upT[j], c_wdn[j // 4]], writes=[bb], signal=(j == 31))
                    P.op("dve", lambda e, sl=sl, half=half, bank=bank: e.tensor_tensor(out=xbs[sl][:, half * 512:(half + 1) * 512], in0=xbs[sl][:, half * 512:(half + 1) * 512],
                                                                                       in1=bank[:], op=ALU.add),
                         reads=[bb], writes=[c_xb[sl]])
                ap_, dp_ = rms_parts(xbs[sl][:], c_xb[sl], gfin_sb, c_gfin, i, 2 + i, inplace=True)
                ap_()

                def fin_store(dp_=dp_, tile=tile, sl=sl):
                    dp_()
                    P.dma("sp", lambda e: e.dma_start(out=out[tile * 128:(tile + 1) * 128, :], in_=xbs[sl][:]), "xb%d" % sl, reads=[c_xb[sl]])
                pending.append(fin_store)

        def phaseb_early_start():
            bar["cur"] = P.barrier(dma=False)
            P.op("pool", lambda e: e.memset(epsb[:], EPS), writes=[c_eps], deps=bar["cur"])
            P.dma("sp", lambda e: e.dma_start(out=gmlp_sb[:], in_=g_mlp[0, :].partition_broadcast(128)), "gmlp", writes=[c_gmlp], deps=bar["cur"])
            load_xb(0)

        for dp_ in prep_norm(0):
            dp_()
        fold0()
        prep_T(0)
        fold2()
        for s in range(9):
            if s == 8:
                state["bar7"] = P.barrier()
            if s == 1:
                late_setup()
            step(s)

        print("phase A sbuf end", A.off, "of", SB_END)
        if stage == "A":
            block = st.enter_context(nc.Block())
            P.replay(block, P.barrier())
            return nc

        print("phase B seq end", seq_end, "of", SB_END)
        bar["cur"] = P.barrier()
        P.dma("sp", lambda e: e.dma_start(out=gfin_sb[:], in_=g_fin[0, :].partition_broadcast(128)), "gfin", writes=[c_gfin], deps=bar["cur"])
        load_xb(1)
        for c in range(8):
            P.dma("sp", lambda e, c=c: e.dma_start(out=w_dn_sb[:, 4 * c:4 * c + 4, :], in_=wdn_bf[:, 4 * c:4 * c + 4, :]),
                  "wdn%d" % c, reads=[c_wdns[c]], writes=[c_wdn[c]], deps=bar["cur"])
        for t in range(NSTEP):
            up(t)
            down(t)
        flush_pending()

        block = st.enter_context(nc.Block())
        P.replay(block, P.barrier())
    return nc


def _bias_tables(rpb):
    specs = [(-3, True), (-2, True), (-2, False), (-1, False), (0, False), (1, False), (2, False), (2, True), (3, True)]
    kc = np.arange(64)[:, None]
    qc = np.arange(64)[None, :]
    c0 = np.clip(qc - 8, 0, 48)
    colok = (kc >= c0) & (kc < c0 + 16)
    dcidx = np.clip(kc - qc + 15, 0, 30)
    tab = np.full((9, 2, 64, 8, 2, 64), NEG, np.float32)
    for t, (rel, full) in enumerate(specs):
        for a in range(2):
            for b in range(2):
                dr = 2 * rel + a - b
                if dr < -7 or dr > 7:
                    continue
                if (not full) and (dr < -4 or dr > 3):
                    continue
                vals = rpb[:, dr + 7, :][:, dcidx]
                blk = np.where(colok[None], vals, np.float32(NEG))
                tab[t, a, :, :, b, :] = blk.transpose(1, 0, 2)
    return np.ascontiguousarray(tab.reshape(9, 128, 1024))


def _inv_counts():
    ic = np.zeros((2, 4, 8), np.float32)
    for g in range(4):
        w = 2 << g
        for i in range(8):
            t = i
            ic[0, g, i] = float(w) / (min(t - w // 2 + w, T) - max(t - w // 2, 0))
            t = T - 8 + i
            ic[1, g, i] = float(w) / (min(t - w // 2 + w, T) - max(t - w // 2, 0))
    return ic.reshape(1, 64)


_NC_CACHE = {}


def kernel(x, norm_mix_g, w_in, w_pool, pool_scale, rpb, w_out, norm_mlp_g, w_up, w_down, final_g, _stage="full"):
    f = lambda a: np.ascontiguousarray(np.asarray(a, dtype=np.float32))
    x = f(x)
    if _stage not in _NC_CACHE:
        _NC_CACHE[_stage] = build_nc(_stage)
    nc = _NC_CACHE[_stage]
    shared = {
        "g_mix": np.ascontiguousarray(f(norm_mix_g).reshape(8, 128).T),
        "g_mlp": f(norm_mlp_g).reshape(1, D),
        "g_fin": f(final_g).reshape(1, D),
        "w_in": f(w_in)[0],
        "w_pool": f(w_pool)[0],
        "pscale": np.ascontiguousarray(f(pool_scale)[0].reshape(4, 128).T),
        "invc": _inv_counts(),
        "btab": _bias_tables(f(rpb)[0]),
        "w_out": f(w_out)[0],
        "w_up": f(w_up)[0],
        "w_down": f(w_down)[0],
        "ident": np.eye(128, dtype=np.float32),
    }
    n = x.shape[0]
    in_maps = [dict(shared, x=x[b]) for b in range(n)]
    res = run_bass_kernel_spmd(nc, in_maps, core_ids=list(range(n)))
    return np.stack([np.asarray(r["out"], dtype=np.float32) for r in res.results], axis=0)
```

```python
import numpy as np
from contextlib import ExitStack
import concourse.bass as bass
import concourse.mybir as mybir
from concourse.bass_utils import run_bass_kernel_spmd

F32 = mybir.dt.float32
BF16 = mybir.dt.bfloat16
AF = mybir.ActivationFunctionType
ALU = mybir.AluOpType

T = 4096
D = 1024
DFF = 4096
NT = 32
EPS = 1e-6
SB_BASE = 16512
SB_END = 229376
NEG = -200.0


class Buf:
    __slots__ = ("w", "r")

    def __init__(self):
        self.w = {}
        self.r = {}


class Prog:
    ENGS = ("pe", "act", "dve", "pool", "sp")

    def __init__(self, nc, stack):
        self.nc = nc
        self.stack = stack
        self.ops = {e: [] for e in self.ENGS}
        self.count = {e: 0 for e in self.ENGS}
        self.waited = {e: {} for e in self.ENGS}
        self.esem = {e: stack.enter_context(nc.semaphore("s_" + e)) for e in self.ENGS if e != "sp"}
        self.dsem = {}
        self.dcount = {}
        self.all_dma = {}

    def _collect(self, reads, writes, deps):
        d = {}

        def add(k, v):
            if d.get(k, 0) < v:
                d[k] = v
        for b in reads:
            for k, v in b.w.items():
                add(k, v)
        for b in writes:
            for k, v in b.w.items():
                add(k, v)
            for k, v in b.r.items():
                add(k, v)
        for t in deps:
            if t is not None:
                add(t[0], t[1])
        return d

    def _waits(self, eng, d):
        w = []
        for key, val in d.items():
            if eng == "pe" and key == "pe":
                continue
            if self.waited[eng].get(key, 0) >= val:
                continue
            self.waited[eng][key] = val
            w.append((key, val))
        return w

    def _note(self, tok, reads, writes):
        k, v = tok
        for b in reads:
            if b.r.get(k, 0) < v:
                b.r[k] = v
        for b in writes:
            b.w = {k: v}
            b.r = {}

    def op(self, eng, fn, reads=(), writes=(), deps=(), signal=True):
        w = self._waits(eng, self._collect(reads, writes, deps))
        if signal:
            self.count[eng] += 1
            tok = (eng, self.count[eng])
            self.ops[eng].append((fn, w, (eng, 1)))
        else:
            assert eng == "pe"
            tok = (eng, self.count[eng] + 1)
            self.ops[eng].append((fn, w, None))
        self._note(tok, reads, writes)
        return tok

    def dma(self, eng, fn, slot, reads=(), writes=(), deps=()):
        if slot not in self.dsem:
            self.dsem[slot] = self.stack.enter_context(self.nc.semaphore("d_" + slot))
            self.dcount[slot] = 0
        w = self._waits(eng, self._collect(reads, writes, deps))
        self.dcount[slot] += 16
        tok = ("D" + slot, self.dcount[slot])
        self.ops[eng].append((fn, w, ("D" + slot, 16)))
        self.all_dma["D" + slot] = self.dcount[slot]
        self._note(tok, reads, writes)
        return tok

    def barrier(self, dma=True):
        toks = [(e, self.count[e]) for e in self.ENGS if e != "sp" and self.count[e] > 0]
        if dma:
            toks += list(self.all_dma.items())
        return toks

    def _sem(self, key):
        if key.startswith("D"):
            return self.dsem[key[1:]]
        return self.esem[key]

    def replay(self, block, final_deps):
        def run(engname):
            def body(e):
                for fn, w, inc in self.ops[engname]:
                    for key, val in w:
                        e.wait_ge(self._sem(key), val)
                    ins = fn(e)
                    if inc is not None:
                        ins.then_inc(self._sem(inc[0]), inc[1])
                if engname == "sp":
                    for key, val in final_deps:
                        e.wait_ge(self._sem(key), val)
            return body

        block.tensor(run("pe"))
        block.scalar(run("act"))
        block.vector(run("dve"))
        block.gpsimd(run("pool"))
        block.sync(run("sp"))


class Alloc:
    def __init__(self, nc, prefix):
        self.nc = nc
        self.prefix = prefix
        self.off = SB_BASE

    def __call__(self, name, shape, dt):
        n = 1
        for s in shape[1:]:
            n *= s
        nbytes = n * (4 if dt == F32 else 2)
        off = (self.off + 31) // 32 * 32
        self.off = off + nbytes
        assert self.off <= SB_END, (name, self.off)
        return self.nc.alloc_sbuf_tensor_at(self.prefix + name, shape, dt, offset=off)


def pair_chunks(m):
    if m == 0:
        return [(0, 4), (1, 5), (2, 7), (3, 8)]
    if m == 1:
        return [(0, 3), (1, 4), (2, 5), (3, 7)]
    if m == 30:
        return [(28, 1), (29, 3), (30, 4), (31, 5)]
    if m == 31:
        return [(28, 0), (29, 1), (30, 3), (31, 4)]
    return [(m - 2, 2), (m - 1, 3), (m, 4), (m + 1, 5), (m + 2, 6)]


def build_nc(stage="full"):
    nc = bass.Bass("TRN2", target_bir_lowering=False)

    def din(name, shape):
        return nc.dram_tensor(name, shape, F32, kind="ExternalInput").ap()

    x = din("x", [T, D])
    g_mix = din("g_mix", [128, 8])
    g_mlp = din("g_mlp", [1, D])
    g_fin = din("g_fin", [1, D])
    w_in = din("w_in", [D, 2048])
    w_pool = din("w_pool", [4, 128, 128])
    pscale = din("pscale", [128, 4])
    invc = din("invc", [1, 64])
    btab = din("btab", [9, 128, 1024])
    w_out = din("w_out", [D, D])
    w_up = din("w_up", [D, DFF])
    w_down = din("w_down", [DFF, D])
    ident_d = din("ident", [128, 128])
    out = nc.dram_tensor("out", [T, D], F32, kind="ExternalOutput").ap()
    x1s = nc.dram_tensor("x1s", [T, D], F32).ap()
    wup_bf = nc.dram_tensor("wup_bf", [128, 8, DFF], BF16).ap()
    wdn_bf = nc.dram_tensor("wdn_bf", [128, 32, D], BF16).ap()

    with ExitStack() as st:
        P = Prog(nc, st)
        big = [nc.alloc_psum_tensor("big%d" % i, [128, 512], F32) for i in range(2)]
        Tb = nc.alloc_psum_tensor("Tb", [128, 1024], BF16)
        Sb = [[nc.alloc_psum_tensor("S%d%d" % (par, s_), [128, 2, 2, 128], F32) for s_ in range(2)] for par in range(2)]
        Vb = nc.alloc_psum_tensor("Vb", [128, 512], F32)
        b_big = [Buf(), Buf()]
        b_Tb = Buf()
        b_Sb = [[Buf(), Buf()], [Buf(), Buf()]]
        b_Vb = Buf()

        B = Alloc(nc, "b_")
        w_up_sb = B("w_up", [128, 8, DFF], BF16)
        c_wup = [Buf() for _ in range(8)]
        c_wups = [Buf() for _ in range(8)]
        c_wdns = [Buf() for _ in range(8)]
        A = Alloc(nc, "a_")
        NXA, NXR = 4, 4
        w_in_sb = A("w_in", [128, 8, 2048], BF16)
        hT = A("hT", [128, 8, 512], BF16)
        xa = A("xa", [128, NXA, D], F32)
        hb = A("hb", [128, 4, D], BF16)
        assert A.off == SB_BASE + 65536
        ident = A("ident", [128, 128], BF16)
        w_out_sb = A("w_out", [128, 8, 1024], BF16)
        w_pool_sb = A("w_pool", [128, 4, 128], BF16)
        pscale_sb = A("pscale", [128, 4], F32)
        gmix_sb = A("gmix", [128, 8], F32)
        invc_sb = A("invc", [128, 2, 4, 8], F32)
        eps_sb = A("eps", [128, 1], F32)
        Ttab = A("Ttab", [128, 9, 8, 128], BF16)
        qT = A("qT", [128, 2, 4, 512], BF16)
        kT = A("kT", [128, 3, 4, 512], BF16)
        Va = A("Va", [128, 3, 4, 8, 65], BF16)
        xr = A("xr", [128, NXR, D], F32)
        ss = A("ss", [128, 4], F32)
        lnv = A("lnv", [128, 4], F32)
        rstd = A("rstd", [128, 4], F32)
        Pb = A("Pb", [128, 2, 6, 4, 128], BF16)
        catT = A("catT", [128, 8, 512], BF16)
        bt = A("bt", [128, 2, 512], BF16)
        rden = A("rden", [128, 2, 4], F32)
        tail_start = A.off
        U = A("U", [128, 2, 4, 528], F32)
        invw = A("invw", [128, 4, 512], F32)
        pooled = A("pooled", [128, 4, 512], BF16)
        TA = A("TA", [128, 528], F32)
        TBt = A("TBt", [128, 528], F32)
        tmp8 = A("tmp8", [128, 8], F32)
        b_invw = Buf()

        b_ident, b_wpool, b_ps, b_gmix, b_invc, b_eps = Buf(), Buf(), Buf(), Buf(), Buf(), Buf()
        b_win = [Buf() for _ in range(4)]
        b_wout = [Buf() for _ in range(8)]
        b_Ttab = [Buf() for _ in range(9)]
        b_stage = [Buf(), Buf()]
        b_q = [[Buf() for _ in range(4)] for _ in range(2)]
        b_k = [[Buf() for _ in range(4)] for _ in range(3)]
        b_v = [[Buf() for _ in range(4)] for _ in range(3)]
        b_U = [[Buf() for _ in range(4)] for _ in range(2)]
        b_hT = [Buf() for _ in range(4)]
        b_xa = [Buf() for _ in range(NXA)]
        b_xr = [Buf() for _ in range(NXR)]
        b_hb = [Buf() for _ in range(4)]
        b_TA, b_TB, b_tmp8 = Buf(), Buf(), Buf()
        b_ss = [Buf() for _ in range(4)]
        b_Pb = [[[Buf(), Buf()] for _ in range(6)] for _ in range(2)]
        b_pooled = [Buf() for _ in range(4)]
        b_cat = [Buf() for _ in range(4)]
        b_catb = [Buf() for _ in range(4)]
        b_bt = [Buf(), Buf()]
        b_rden = [Buf(), Buf()]
        b_x1s = [Buf() for _ in range(NT)]

        P.dma("pool", lambda e: e.dma_start(out=ident[:], in_=ident_d), "ident", writes=[b_ident])
        P.op("pool", lambda e: e.memset(eps_sb[:], EPS), writes=[b_eps])
        for g in range(4):
            P.op("pool", lambda e, g=g: e.memset(invw[:, g, :], 1.0 / (2 << g)), writes=[b_invw])
        P.op("pool", lambda e: e.memset(Va[:, :, :, :, 64:65], 1.0), writes=[b for r in b_v for b in r])
        P.op("pool", lambda e: e.memset(U[:, 0, :, 0:8], 0.0), writes=b_U[0])
        P.dma("sp", lambda e: e.dma_start(out=gmix_sb[:], in_=g_mix), "gmix", writes=[b_gmix])
        P.dma("sp", lambda e: e.dma_start(out=pscale_sb[:], in_=pscale), "pscale", writes=[b_ps])
        P.dma("sp", lambda e: e.dma_start(out=invc_sb[:].rearrange("p a g t -> p (a g t)"), in_=invc[0, :].partition_broadcast(128)), "invc", writes=[b_invc])
        for i in range(4):
            P.dma("sp", lambda e, i=i: e.dma_start(out=xa[:, i, :], in_=x[i * 128:(i + 1) * 128, :]), "xa%d" % i, writes=[b_xa[i]])
        xr_stage = xr[:].rearrange("p s (c n) -> p (s c) n", c=2)

        def load_win_sw(blk):
            P.dma("pool", lambda e: e.dma_start(out=w_in_sb[:, :, blk * 512:(blk + 1) * 512],
                                                in_=w_in[:, blk * 512:(blk + 1) * 512].rearrange("(c p) n -> p c n", p=128)),
                  "win%d" % blk, writes=[b_win[blk]])

            def fold():
                for kc in range(8):
                    P.op("dve", lambda e, kc=kc: e.tensor_scalar(out=w_in_sb[:, kc, blk * 512:(blk + 1) * 512], in0=w_in_sb[:, kc, blk * 512:(blk + 1) * 512],
                                                                 scalar1=gmix_sb[:, kc:kc + 1], scalar2=None, op0=ALU.mult),
                         reads=[b_gmix], writes=[b_win[blk]])
            return fold

        def load_win_hw(blk):
            P.dma("sp", lambda e: e.dma_start(out=xr_stage, in_=w_in[:, blk * 512:(blk + 1) * 512].rearrange("(c p) n -> p c n", p=128)),
                  "stage", writes=b_xr)

            def fold():
                for kc in range(8):
                    P.op("dve", lambda e, kc=kc: e.tensor_scalar(out=w_in_sb[:, kc, blk * 512:(blk + 1) * 512], in0=xr_stage[:, kc, :],
                                                                 scalar1=gmix_sb[:, kc:kc + 1], scalar2=None, op0=ALU.mult),
                         reads=[b_gmix] + b_xr, writes=[b_win[blk]])
            return fold
        fold0 = load_win_hw(0)
        fold2 = load_win_sw(2)
        fold1 = load_win_sw(1)
        P.dma("pool", lambda e: e.dma_start(out=w_pool_sb[:], in_=w_pool.rearrange("g c d -> c g d")), "wpool", writes=[b_wpool])
        state = {"nb": 0, "sset": 0, "pb": 0, "ev": 0}

        def next_big():
            i = state["nb"] % 2
            state["nb"] += 1
            return big[i], b_big[i]

        def copy_evac(out_ap, in_ap, reads, writes, eng=None, scale=None):
            if eng is None:
                eng = "dve"
            if eng == "act":
                if scale is not None:
                    return P.op("act", lambda e: e.activation(out=out_ap, in_=in_ap, func=AF.Copy, scale=scale), reads=reads, writes=writes)
                return P.op("act", lambda e: e.copy(out=out_ap, in_=in_ap), reads=reads, writes=writes)
            if scale is not None:
                return P.op("dve", lambda e: e.tensor_scalar(out=out_ap, in0=in_ap, scalar1=scale, scalar2=None, op0=ALU.mult), reads=reads, writes=writes)
            return P.op("dve", lambda e: e.tensor_copy(out=out_ap, in_=in_ap), reads=reads, writes=writes)

        def rms_h(xt_ap, b_x, hslot, col):
            P.op("act", lambda e: e.activation(out=hb[:, hslot, :], in_=xt_ap, func=AF.Square, accum_out=ss[:, col:col + 1]),
                 reads=[b_x], writes=[b_hb[hslot], b_ss[col]])
            P.op("act", lambda e: e.activation(out=lnv[:, col:col + 1], in_=ss[:, col:col + 1], func=AF.Ln, scale=1.0 / D, bias=eps_sb[:, 0:1]),
                 reads=[b_eps], writes=[b_ss[col]])
            P.op("act", lambda e: e.activation(out=rstd[:, col:col + 1], in_=lnv[:, col:col + 1], func=AF.Exp, scale=-0.5),
                 writes=[b_ss[col]])

            def dve_part():
                P.op("dve", lambda e: e.tensor_scalar(out=hb[:, hslot, :], in0=xt_ap, scalar1=rstd[:, col:col + 1], scalar2=None, op0=ALU.mult),
                     reads=[b_x, b_ss[col]], writes=[b_hb[hslot]])
            return dve_part

        def load_xa(s):
            for i in range(4):
                tile = 4 * s + i
                sl = tile % NXA
                P.dma("sp", lambda e, tile=tile, sl=sl: e.dma_start(out=xa[:, sl, :], in_=x[tile * 128:(tile + 1) * 128, :]), "xa%d" % sl, writes=[b_xa[sl]])

        def load_xr(s):
            for i in range(4):
                tile = 4 * s + i
                sl = tile % NXR
                P.dma("sp", lambda e, tile=tile, sl=sl: e.dma_start(out=xr[:, sl, :], in_=x[tile * 128:(tile + 1) * 128, :]), "xr%d" % sl, writes=[b_xr[sl]])

        def prep_norm(s, tiles=(0, 1, 2, 3)):
            dparts = []
            for i in tiles:
                tile = 4 * s + i
                sl = tile % NXA
                dparts.append(rms_h(xa[:, sl, :], b_xa[sl], i, i))
            return dparts

        def prep_T_tile(i):
            for kc in range(8):
                P.op("pe", lambda e, kc=kc, i=i: e.transpose(out=Tb[:, kc * 128:(kc + 1) * 128], in_=hb[:, i, kc * 128:(kc + 1) * 128], identity=ident[:]),
                     reads=[b_hb[i], b_ident], writes=[b_Tb], signal=(kc == 7))
            copy_evac(hT[:, :, i * 128:(i + 1) * 128], Tb[:].rearrange("p (k n) -> p k n", k=8), [b_Tb], [b_hT[i]])

        def prep_T_halo(s):
            su = s % 2
            if s >= 1:
                P.op("pool", lambda e: e.tensor_copy(out=U[:, su, :, 0:8], in_=U[:, 1 - su, :, 512:520]), reads=b_U[1 - su], writes=b_U[su])

        def prep_T(s):
            for i in range(4):
                prep_T_tile(i)
            prep_T_halo(s)

        def proj_groups(s):
            sq, sk, su = s % 2, s % 3, s % 2

            def fm_group(oc):
                def run():
                    bank, bb = next_big()
                    for kc in range(8):
                        P.op("pe", lambda e, kc=kc, bank=bank: e.matmul(bank[:], lhsT=w_in_sb[:, kc, oc * 128:(oc + 1) * 128], rhs=hT[:, kc, :],
                                                                        start=(kc == 0), stop=(kc == 7)),
                             reads=[b_win[oc // 4]] + b_hT, writes=[bb], signal=(kc == 7))
                    j = oc % 4
                    if oc < 4:
                        copy_evac(U[:, su, j, 8:520], bank[:], [bb], [b_U[su][j]])
                    elif oc < 8:
                        copy_evac(qT[:, sq, j, :], bank[:], [bb], [b_q[sq][j]], scale=0.125)
                    else:
                        copy_evac(kT[:, sk, j, :], bank[:], [bb], [b_k[sk][j]])
                return run

            def v_group(i):
                def run():
                    bank, bb = next_big()
                    for kc in range(8):
                        P.op("pe", lambda e, kc=kc, bank=bank: e.matmul(bank[:], lhsT=hT[:, kc, i * 128:(i + 1) * 128], rhs=w_in_sb[:, kc, 1536:2048],
                                                                        start=(kc == 0), stop=(kc == 7)),
                             reads=[b_win[3], b_hT[i]], writes=[bb], signal=(kc == 7))
                    copy_evac(Va[:, sk, i, :, 0:64], bank[:].rearrange("p (h d) -> p h d", h=8), [bb], [b_v[sk][i]])
                return run
            return [fm_group(oc) for oc in range(4)] + [fm_group(oc) for oc in range(8, 12)] + [v_group(i) for i in range(4)] + [fm_group(oc) for oc in range(4, 8)]

        def proj_halo(s):
            su = s % 2
            if s >= 1:
                P.op("pool", lambda e: e.tensor_copy(out=U[:, 1 - su, :, 520:528], in_=U[:, su, :, 8:16]), reads=b_U[su], writes=b_U[1 - su])
            if s == 7:
                P.op("pool", lambda e: e.memset(U[:, su, :, 520:528], 0.0), writes=b_U[su])

        def unit_fills(m, Q, pb):
            chunks = pair_chunks(m)
            sq = (m // 4) % 2
            qc0 = (m % 4) * 128
            ncp = (len(chunks) + 1) // 2

            def mk(cp):
                def run():
                    sset = state["sset"] % 2
                    state["sset"] += 1
                    cis = [ci for ci in (2 * cp, 2 * cp + 1) if ci < len(chunks)]
                    n = len(cis)
                    tids = [chunks[ci][1] for ci in cis]
                    tstep = (tids[1] - tids[0]) if n == 2 else 1
                    for par in range(2):
                        tab_ap = Ttab[:, tids[0]:tids[-1] + 1:tstep, 4 * Q + par:4 * Q + par + 3:2, :]
                        P.op("pe", lambda e, par=par, tab_ap=tab_ap: e.matmul(Sb[par][sset][:, 0:n, :, :], lhsT=ident[:], rhs=tab_ap, start=True, stop=False,
                                                                             skip_group_check=True),
                             reads=[b_ident] + [b_Ttab[t_] for t_ in tids], writes=[b_Sb[par][sset]], signal=False)
                    for ci in cis:
                        c, tid = chunks[ci]
                        sk = (c // 4) % 3
                        kc0 = (c % 4) * 128
                        for h4 in range(4):
                            h = 4 * Q + h4
                            j, par = h // 2, h % 2
                            last = (ci == cis[-1] and h4 >= 2)
                            P.op("pe", lambda e, par=par, ci=ci, h4=h4, sk=sk, j=j, kc0=kc0:
                                 e.matmul(Sb[par][sset][:, ci % 2, h4 // 2, :],
                                          lhsT=kT[par * 64:(par + 1) * 64, sk, j, kc0:kc0 + 128],
                                          rhs=qT[par * 64:(par + 1) * 64, sq, j, qc0:qc0 + 128], start=False, stop=True, skip_group_check=True),
                                 reads=[b_k[sk][j], b_q[sq][j]], writes=[b_Sb[par][sset]], signal=last)
                    for par in range(2):
                        P.op("act", lambda e, par=par:
                             e.activation(out=Pb[:, pb, 2 * cp:2 * cp + n, par:4:2, :], in_=Sb[par][sset][:, 0:n, :, :], func=AF.Exp),
                             reads=[b_Sb[par][sset]], writes=[b_Pb[pb][ci_][par] for ci_ in cis])
                return run
            return [mk(cp) for cp in range(ncp)]

        def attn_pv(m, Q, pb):
            chunks = pair_chunks(m)
            bsl = m % 2
            Vv = Vb[:, 0:260].rearrange("p (h d) -> p h d", h=4)
            for h4 in range(4):
                for ci, (c, tid) in enumerate(chunks):
                    sk = (c // 4) % 3
                    P.op("pe", lambda e, h4=h4, ci=ci, c=c, sk=sk, pb=pb, Q=Q:
                         e.matmul(Vb[:, h4 * 65:(h4 + 1) * 65], lhsT=Pb[:, pb, ci, h4, :], rhs=Va[:, sk, c % 4, 4 * Q + h4, :],
                                  start=(ci == 0), stop=(ci == len(chunks) - 1)),
                         reads=[b_Pb[pb][ci][h4 % 2], b_v[sk][c % 4]], writes=[b_Vb], signal=(h4 == 3 and ci == len(chunks) - 1))
            P.op("dve", lambda e: e.reciprocal(out=rden[:, bsl, :], in_=Vv[:, :, 64]), reads=[b_Vb], writes=[b_rden[bsl]])
            P.op("dve", lambda e: e.tensor_tensor(out=bt[:, bsl, Q * 256:(Q + 1) * 256].rearrange("p (h d) -> p h d", h=4), in0=Vv[:, :, 0:64],
                                                  in1=rden[:, bsl, :].unsqueeze(2).broadcast_to([128, 4, 64]), op=ALU.mult),
                 reads=[b_Vb, b_rden[bsl]], writes=[b_bt[bsl]])

        def attn_finish(m):
            bsl = m % 2
            i = m % 4
            for j in range(4):
                P.op("pe", lambda e, j=j: e.transpose(out=Tb[:, j * 128:(j + 1) * 128], in_=bt[:, bsl, j * 128:(j + 1) * 128], identity=ident[:]),
                     reads=[b_bt[bsl], b_ident], writes=[b_Tb], signal=(j == 3))
            copy_evac(catT[:, 4:8, i * 128:(i + 1) * 128], Tb[:, 0:512].rearrange("p (k n) -> p k n", k=4), [b_Tb], [b_catb[i]], eng="act")

        def fin_pool(s):
            su = s % 2
            for g in range(4):
                w = 2 << g
                Ug = U[:, su, g, :]
                P.op("pool", lambda e, Ug=Ug: e.tensor_tensor(out=TA[:, 1:528], in0=Ug[:, 0:527], in1=Ug[:, 1:528], op=ALU.add), reads=[b_U[su][g]], writes=[b_TA])
                if g >= 1:
                    P.op("pool", lambda e: e.tensor_tensor(out=TBt[:, 2:527], in0=TA[:, 1:526], in1=TA[:, 3:528], op=ALU.add), reads=[b_TA], writes=[b_TB])
                if g >= 2:
                    P.op("pool", lambda e: e.tensor_tensor(out=TA[:, 4:525], in0=TBt[:, 2:523], in1=TBt[:, 6:527], op=ALU.add), reads=[b_TB], writes=[b_TA])
                if g >= 3:
                    P.op("pool", lambda e: e.tensor_tensor(out=TBt[:, 8:521], in0=TA[:, 4:517], in1=TA[:, 12:525], op=ALU.add), reads=[b_TA], writes=[b_TB])
                L, bL = (TA, b_TA) if g in (0, 2) else (TBt, b_TB)
                P.op("pool", lambda e, L=L, g=g: e.tensor_tensor(out=L[:, 8:520], in0=L[:, 8:520], in1=invw[:, g, :], op=ALU.mult),
                     reads=[b_invw], writes=[bL])
                P.op("pool", lambda e, L=L, Ug=Ug, g=g: e.tensor_tensor(out=pooled[:, g, :], in0=L[:, 8:520], in1=Ug[:, 8:520], op=ALU.subtract),
                     reads=[bL, b_U[su][g]], writes=[b_pooled[g]])
                if s == 0 or s == 7:
                    a = 0 if s == 0 else 1
                    lo = 8 if s == 0 else 512
                    P.op("pool", lambda e, L=L, a=a, lo=lo, g=g: e.tensor_tensor(out=tmp8[:], in0=L[:, lo:lo + 8], in1=invc_sb[:, a, g, :], op=ALU.mult),
                         reads=[bL, b_invc], writes=[b_tmp8])
                    P.op("pool", lambda e, Ug=Ug, lo=lo, g=g: e.tensor_tensor(out=pooled[:, g, lo - 8:lo], in0=tmp8[:], in1=Ug[:, lo:lo + 8], op=ALU.subtract),
                         reads=[b_tmp8, b_U[su][g]], writes=[b_pooled[g]])

        def fin_poolmm(s):
            for g in range(4):
                bank, bb = next_big()
                P.op("pe", lambda e, g=g, bank=bank: e.matmul(bank[:], lhsT=w_pool_sb[:, g, :], rhs=pooled[:, g, :], start=True, stop=True),
                     reads=[b_wpool, b_pooled[g]], writes=[bb])
                P.op("act", lambda e, g=g, bank=bank: e.activation(out=catT[:, g, :], in_=bank[:], func=AF.Copy, scale=pscale_sb[:, g:g + 1]),
                     reads=[bb, b_ps], writes=[b_cat[g]])

        def fin_wout_half(s, i, half):
            tile = 4 * s + i
            sl = tile % NXR
            bank, bb = next_big()
            for c in range(8):
                P.op("pe", lambda e, c=c, bank=bank: e.matmul(bank[:], lhsT=catT[:, c, i * 128:(i + 1) * 128],
                                                              rhs=w_out_sb[:, c, half * 512:(half + 1) * 512], start=(c == 0), stop=(c == 7)),
                     reads=[b_wout[c], b_cat[c % 4] if c < 4 else b_catb[i]], writes=[bb], signal=(c == 7))
            P.op("dve", lambda e, bank=bank: e.tensor_tensor(out=xr[:, sl, half * 512:(half + 1) * 512], in0=xr[:, sl, half * 512:(half + 1) * 512],
                                                             in1=bank[:], op=ALU.add),
                 reads=[bb], writes=[b_xr[sl]])
            if half == 1:
                dst = out if stage == "A" else x1s
                P.dma("sp", lambda e: e.dma_start(out=dst[tile * 128:(tile + 1) * 128, :], in_=xr[:, sl, :]), "xr%d" % sl,
                      reads=[b_xr[sl]], writes=[b_x1s[tile]])

        def fin_wout(s, i):
            fin_wout_half(s, i, 0)
            fin_wout_half(s, i, 1)

        def late_setup():
            for t in range(9):
                sl = t % NXR
                P.dma("sp", lambda e, t=t, sl=sl: e.dma_start(out=xr[:, sl, :], in_=btab[t]), "stg%d" % sl, writes=[b_xr[sl]])
                if t % 2 == 0:
                    P.op("act", lambda e, t=t, sl=sl: e.copy(out=Ttab[:, t, :, :], in_=xr[:, sl, :].rearrange("p (h q) -> p h q", h=8)),
                         reads=[b_xr[sl]], writes=[b_Ttab[t]])
                else:
                    P.op("dve", lambda e, t=t, sl=sl: e.tensor_copy(out=Ttab[:, t, :, :], in_=xr[:, sl, :].rearrange("p (h q) -> p h q", h=8)),
                         reads=[b_xr[sl]], writes=[b_Ttab[t]])
            for c in range(8):
                P.dma("pool", lambda e, c=c: e.dma_start(out=w_out_sb[:, c, :], in_=w_out[c * 128:(c + 1) * 128, :]), "wout%d" % c, writes=[b_wout[c]])

        def precast(lo, hi):
            for c in range(lo, hi):
                if c < 8:
                    P.dma("pool", lambda e, c=c: e.dma_start(out=wup_bf[:, c, :], in_=w_up[c * 128:(c + 1) * 128, :]), "pcu%d" % c, writes=[c_wups[c]])
                else:
                    c2 = c - 8
                    P.dma("pool", lambda e, c2=c2: e.dma_start(out=wdn_bf[:, 4 * c2:4 * c2 + 4, :],
                                                               in_=w_down[c2 * 512:(c2 + 1) * 512, :].rearrange("(j p) d -> p j d", p=128)),
                          "pcd%d" % c2, writes=[c_wdns[c2]])

        def step(s):
            do_proj = s < 8
            do_fin = s >= 1
            if do_fin:
                load_xr(s - 1)
            filler = []
            if do_proj:
                if s + 1 < 8:
                    load_xa(s + 1)
                groups = proj_groups(s)
                for gfn in groups[:4]:
                    gfn()
                filler = groups[4:]
                proj_halo(s)
            if not do_fin:
                fold3 = load_win_hw(3)
                for gfn in filler[0:4]:
                    gfn()
                fold3()
                for gfn in filler[4:8]:
                    gfn()
                fold1()
                for gfn in filler[8:]:
                    gfn()
                for dp_ in prep_norm(s + 1):
                    dp_()
                prep_T(s + 1)
                return
            fin_pool(s - 1)
            if 2 <= s <= 5:
                precast(4 * (s - 2), 4 * (s - 1))
            popped = [0]
            nfill = [0]

            def fill(n=1):
                for _ in range(n):
                    if filler:
                        filler.pop(0)()
                        popped[0] += 1
            units = [(4 * (s - 1) + i, Q) for i in range(4) for Q in range(2)]
            pend_pv = None
            pend_fin = None
            pend_fin2 = [None]
            for k, (m, Q) in enumerate(units):
                pb = state["pb"] % 2
                state["pb"] += 1
                if k == 4 and do_proj:
                    while popped[0] < 8:
                        fill(1)
                fills = unit_fills(m, Q, pb)
                for f, ff in enumerate(fills):
                    ff()
                    if nfill[0] < 8 or nfill[0] % 2 == 0:
                        fill(1)
                    nfill[0] += 1
                    if f == 1 and pend_pv is not None:
                        attn_pv(*pend_pv)
                        if pend_pv[1] == 1:
                            pend_fin = pend_pv[0]
                        pend_pv = None
                    if f == 0 and pend_fin2[0] is not None:
                        attn_finish(pend_fin2[0])
                        filler.append(lambda i=pend_fin2[0] % 4: fin_wout_half(s - 1, i, 0))
                        filler.append(lambda i=pend_fin2[0] % 4: fin_wout_half(s - 1, i, 1))
                        pend_fin2[0] = None
                    if f == len(fills) - 1 and pend_fin is not None:
                        pend_fin2[0] = pend_fin
                        pend_fin = None
                if k == (2 if s == 8 else 3):
                    fin_poolmm(s - 1)
                    if s == 8 and stage != "A":
                        phaseb_early_start()
                    if s == 8:
                        for c in range(8):
                            P.dma("sp", lambda e, c=c: e.dma_start(out=w_up_sb[:, c, :], in_=wup_bf[:, c, :]), "wup%d" % c, reads=[c_wups[c]], writes=[c_wup[c]], deps=state["bar7"])
                if s == 8 and stage != "A":
                    if k == 5:
                        state["dparts2"] = prep_norm2(0)
                    if k == 7:
                        for dp_ in state["dparts2"]:
                            dp_()
                if 1 <= k <= 4 and s + 1 < 8:
                    state["dp%d" % (k - 1)] = prep_norm(s + 1, (k - 1,))
                if 3 <= k <= 6 and s + 1 < 8:
                    for dp_ in state["dp%d" % (k - 3)]:
                        dp_()
                pend_pv = (m, Q, pb)
            if pend_fin2[0] is not None:
                attn_finish(pend_fin2[0])
                filler.append(lambda i=pend_fin2[0] % 4: fin_wout_half(s - 1, i, 0))
                filler.append(lambda i=pend_fin2[0] % 4: fin_wout_half(s - 1, i, 1))
                pend_fin2[0] = None
            fill(len(filler))
            nxt = s + 1 < 8
            last8 = (s == 8 and stage != "A")
            if nxt:
                prep_T_tile(0)
            if last8:
                prep_T2(0, 0)
            attn_pv(*pend_pv)
            if nxt:
                prep_T_tile(1)
                prep_T_tile(2)
            if last8:
                prep_T2(0, 1)
            attn_finish(pend_pv[0])
            if nxt:
                prep_T_tile(3)
                prep_T_halo(s + 1)
            fin_wout(s - 1, pend_pv[0] % 4)

        bar = {"cur": None}
        NXB = 6
        identb = B("ident", [128, 128], BF16)
        w_dn_sb = B("w_dn", [128, 32, D], BF16)
        upT = B("upT", [128, 32, 256], BF16)
        xbs = [None, None] + [B("xb%d" % i, [128, D], F32) for i in range(2, NXB)]
        rt = B("rt", [128, 3, 256], F32)
        hT2p = [None, B("hT2_1", [128, 8, 256], BF16)]
        gfin_sb = B("gfin", [128, D], F32)
        seq_end = B.off

        class TopAlloc:
            def __init__(self):
                self.off = SB_END - 256

            def __call__(self, name, shape, dt):
                n = 1
                for d_ in shape[1:]:
                    n *= d_
                nbytes = n * (4 if dt == F32 else 2)
                self.off = (self.off - nbytes) // 32 * 32
                return nc.alloc_sbuf_tensor_at("b_" + name, shape, dt, offset=self.off)
        Tp = TopAlloc()
        xbs[0] = Tp("xb0", [128, D], F32)
        xbs[1] = Tp("xb1", [128, D], F32)
        xs2 = Tp("xs2", [128, 2, D], F32)
        hb2 = Tp("hb2", [128, 2, D], BF16)
        hT2p[0] = Tp("hT2_0", [128, 8, 256], BF16)
        gmlp_sb = Tp("gmlp", [128, D], F32)
        jb = Tp("jb", [128, D], BF16)
        epsb = Tp("eps", [128, 1], F32)
        ss2 = Tp("ss2", [128, 4], F32)
        lnv2 = Tp("lnv2", [128, 4], F32)
        rstd2 = Tp("rstd2", [128, 4], F32)
        assert Tp.off >= tail_start and Tp.off >= seq_end, (Tp.off, tail_start, seq_end)
        print("phase B seq end", seq_end, "early start", Tp.off, "tail_start", tail_start)

        c_ident, c_gmlp, c_gfin, c_eps = b_ident, Buf(), Buf(), Buf()
        c_xs2 = [Buf(), Buf()]
        c_jb = Buf()
        c_wdn = [Buf() for _ in range(8)]
        c_upT = [Buf() for _ in range(32)]
        c_hT2 = [[Buf(), Buf()], [Buf(), Buf()]]
        c_xb = [Buf() for _ in range(NXB)]
        c_hb2 = [Buf(), Buf()]
        c_ss2 = [Buf() for _ in range(4)]
        c_rt = [Buf() for _ in range(3)]
        upbank = [Sb[0][0], Sb[0][1], Sb[1][0]]
        c_upbank = [b_Sb[0][0], b_Sb[0][1], b_Sb[1][0]]

        assert nc.lookup_mloc(identb).addr == nc.lookup_mloc(ident).addr
        identb = ident

        def rms_parts(xt_ap, b_x, g_sb, b_g, hslot, col, inplace=False):
            def act_part():
                jout, jbuf = (jb[:], c_jb) if inplace else (hb2[:, hslot, :], c_hb2[hslot])
                P.op("act", lambda e: e.activation(out=jout, in_=xt_ap, func=AF.Square, accum_out=ss2[:, col:col + 1]),
                     reads=[b_x], writes=[jbuf, c_ss2[col]], deps=bar["cur"])
                P.op("act", lambda e: e.activation(out=lnv2[:, col:col + 1], in_=ss2[:, col:col + 1], func=AF.Ln, scale=1.0 / D, bias=epsb[:, 0:1]),
                     reads=[c_eps], writes=[c_ss2[col]])
                P.op("act", lambda e: e.activation(out=rstd2[:, col:col + 1], in_=lnv2[:, col:col + 1], func=AF.Exp, scale=-0.5),
                     writes=[c_ss2[col]])
                if not inplace:
                    P.op("act", lambda e: e.activation(out=xs2[:, hslot, :], in_=xt_ap, func=AF.Copy, scale=rstd2[:, col:col + 1]),
                         reads=[b_x, c_ss2[col]], writes=[c_xs2[hslot]])
                else:
                    P.op("act", lambda e: e.activation(out=xt_ap, in_=xt_ap, func=AF.Copy, scale=rstd2[:, col:col + 1]),
                         reads=[c_ss2[col]], writes=[b_x])

            def dve_part():
                if not inplace:
                    P.op("dve", lambda e: e.tensor_tensor(out=hb2[:, hslot, :], in0=xs2[:, hslot, :], in1=g_sb[:], op=ALU.mult),
                         reads=[c_xs2[hslot], b_g], writes=[c_hb2[hslot]], deps=bar["cur"])
                else:
                    P.op("dve", lambda e: e.tensor_tensor(out=xt_ap, in0=xt_ap, in1=g_sb[:], op=ALU.mult),
                         reads=[b_g], writes=[b_x])
            return act_part, dve_part

        def load_xb(t):
            for i in range(2):
                tile = 2 * t + i
                sl = tile % NXB
                P.dma("sp", lambda e, tile=tile, sl=sl: e.dma_start(out=xbs[sl][:], in_=x1s[tile * 128:(tile + 1) * 128, :]), "xb%d" % sl,
                      reads=[b_x1s[tile]], writes=[c_xb[sl]], deps=bar["cur"])

        NSTEP = 16
        pending = []

        def flush_pending():
            while pending:
                pending.pop(0)()

        def prep_norm2(t):
            dparts = []
            for i in range(2):
                tile = 2 * t + i
                sl = tile % NXB
                ap_, dp_ = rms_parts(xbs[sl][:], c_xb[sl], gmlp_sb, c_gmlp, i, i)
                ap_()
                dparts.append(dp_)
            return dparts

        def prep_T2(t, i):
            hs = t % 2
            for kc in range(8):
                P.op("pe", lambda e, kc=kc, i=i: e.transpose(out=Tb[:, kc * 128:(kc + 1) * 128], in_=hb2[:, i, kc * 128:(kc + 1) * 128], identity=identb[:]),
                     reads=[c_hb2[i], c_ident], writes=[b_Tb], signal=(kc == 7), deps=bar["cur"])
            P.op("dve", lambda e: e.tensor_copy(out=hT2p[hs][:, :, i * 128:(i + 1) * 128], in_=Tb[:].rearrange("p (k n) -> p k n", k=8)),
                 reads=[b_Tb], writes=[c_hT2[hs][i]], deps=bar["cur"])

        def up(t):
            hs = t % 2
            for j in range(32):
                if j == 4:
                    flush_pending()
                if j == 5 and t + 2 < NSTEP:
                    load_xb(t + 2)
                if t + 1 < NSTEP:
                    if j == 5:
                        state["dparts"] = prep_norm2(t + 1)
                    if j == 12:
                        state["dparts"][0]()
                    if j == 15:
                        state["dparts"][1]()
                    if j == 19:
                        prep_T2(t + 1, 0)
                    if j == 26:
                        prep_T2(t + 1, 1)
                bk, bbk = upbank[j % 3], c_upbank[j % 3]
                bkv = bk[:].rearrange("p a b n -> p (a b n)")[:, 0:256]
                for kc in range(8):
                    P.op("pe", lambda e, kc=kc, j=j, bkv=bkv: e.matmul(bkv, lhsT=w_up_sb[:, kc, j * 128:(j + 1) * 128], rhs=hT2p[hs][:, kc, :], start=(kc == 0), stop=(kc == 7)),
                         reads=[c_wup[kc]] + c_hT2[hs], writes=[bbk], signal=(kc == 7))
                r = j % 3
                if j % 4 == 1 and not (2 <= j <= 10):
                    P.op("act", lambda e, r=r, bkv=bkv: e.activation(out=rt[:, r, :], in_=bkv, func=AF.Relu), reads=[bbk], writes=[c_rt[r]], deps=bar["cur"])
                else:
                    P.op("dve", lambda e, r=r, bkv=bkv: e.tensor_scalar(out=rt[:, r, :], in0=bkv, scalar1=0.0, scalar2=None, op0=ALU.max), reads=[bbk], writes=[c_rt[r]], deps=bar["cur"])
                P.op("pool", lambda e, r=r, j=j: e.tensor_tensor(out=upT[:, j, :], in0=rt[:, r, :], in1=rt[:, r, :], op=ALU.mult), reads=[c_rt[r]], writes=[c_upT[j]], deps=bar["cur"])

        def down(t):
            for i in range(2):
                tile = 2 * t + i
                sl = tile % NXB
                for half in range(2):
                    bank, bb = next_big()
                    if i == 1 and half == 1:
                        flush_pending()
                    for j in range(32):
                        P.op("pe", lambda e, j=j, i=i, half=half, bank=bank: e.matmul(bank[:], lhsT=upT[:, j, i * 128:(i + 1) * 128],
                                                                                       rhs=w_dn_sb[:, j, half * 512:(half + 1) * 512], start=(j == 0), stop=(j == 31)),
                             reads=[c_upT[j], c_wdn[j // 4]], writes=[bb], signal=(j == 31))
                    P.op("dve", lambda e, sl=sl, half=half, bank=bank: e.tensor_tensor(out=xbs[sl][:, half * 512:(half + 1) * 512], in0=xbs[sl][:, half * 512:(half + 1) * 512],
                                                                                       in1=bank[:], op=ALU.add),
                         reads=[bb], writes=[c_xb[sl]])
                ap_, dp_ = rms_parts(xbs[sl][:], c_xb[sl], gfin_sb, c_gfin, i, 2 + i, inplace=True)
                ap_()

                def fin_store(dp_=dp_, tile=tile, sl=sl):
                    dp_()
                    P.dma("sp", lambda e: e.dma_start(out=out[tile * 128:(tile + 1) * 128, :], in_=xbs[sl][:]), "xb%d" % sl, reads=[c_xb[sl]])
                pending.append(fin_store)

        def phaseb_early_start():
            bar["cur"] = P.barrier(dma=False)
            P.op("pool", lambda e: e.memset(epsb[:], EPS), writes=[c_eps], deps=bar["cur"])
            P.dma("sp", lambda e: e.dma_start(out=gmlp_sb[:], in_=g_mlp[0, :].partition_broadcast(128)), "gmlp", writes=[c_gmlp], deps=bar["cur"])
            load_xb(0)

        for dp_ in prep_norm(0):
            dp_()
        fold0()
        prep_T(0)
        fold2()
        for s in range(9):
            if s == 8:
                state["bar7"] = P.barrier()
            if s == 1:
                late_setup()
            step(s)

        print("phase A sbuf end", A.off, "of", SB_END)
        if stage == "A":
            block = st.enter_context(nc.Block())
            P.replay(block, P.barrier())
            return nc

        print("phase B seq end", seq_end, "of", SB_END)
        bar["cur"] = P.barrier()
        P.dma("sp", lambda e: e.dma_start(out=gfin_sb[:], in_=g_fin[0, :].partition_broadcast(128)), "gfin", writes=[c_gfin], deps=bar["cur"])
        load_xb(1)
        for c in range(8):
            P.dma("sp", lambda e, c=c: e.dma_start(out=w_dn_sb[:, 4 * c:4 * c + 4, :], in_=wdn_bf[:, 4 * c:4 * c + 4, :]),
                  "wdn%d" % c, reads=[c_wdns[c]], writes=[c_wdn[c]], deps=bar["cur"])
        for t in range(NSTEP):
            up(t)
            down(t)
        flush_pending()

        block = st.enter_context(nc.Block())
        P.replay(block, P.barrier())
    return nc


def _bias_tables(rpb):
    specs = [(-3, True), (-2, True), (-2, False), (-1, False), (0, False), (1, False), (2, False), (2, True), (3, True)]
    kc = np.arange(64)[:, None]
    qc = np.arange(64)[None, :]
    c0 = np.clip(qc - 8, 0, 48)
    colok = (kc >= c0) & (kc < c0 + 16)
    dcidx = np.clip(kc - qc + 15, 0, 30)
    tab = np.full((9, 2, 64, 8, 2, 64), NEG, np.float32)
    for t, (rel, full) in enumerate(specs):
        for a in range(2):
            for b in range(2):
                dr = 2 * rel + a - b
                if dr < -7 or dr > 7:
                    continue
                if (not full) and (dr < -4 or dr > 3):
                    continue
                vals = rpb[:, dr + 7, :][:, dcidx]
                blk = np.where(colok[None], vals, np.float32(NEG))
                tab[t, a, :, :, b, :] = blk.transpose(1, 0, 2)
    return np.ascontiguousarray(tab.reshape(9, 128, 1024))


def _inv_counts():
    ic = np.zeros((2, 4, 8), np.float32)
    for g in range(4):
        w = 2 << g
        for i in range(8):
            t = i
            ic[0, g, i] = float(w) / (min(t - w // 2 + w, T) - max(t - w // 2, 0))
            t = T - 8 + i
            ic[1, g, i] = float(w) / (min(t - w // 2 + w, T) - max(t - w // 2, 0))
    return ic.reshape(1, 64)


_NC_CACHE = {}


def kernel(x, norm_mix_g, w_in, w_pool, pool_scale, rpb, w_out, norm_mlp_g, w_up, w_down, final_g, _stage="full"):
    f = lambda a: np.ascontiguousarray(np.asarray(a, dtype=np.float32))
    x = f(x)
    if _stage not in _NC_CACHE:
        _NC_CACHE[_stage] = build_nc(_stage)
    nc = _NC_CACHE[_stage]
    shared = {
        "g_mix": np.ascontiguousarray(f(norm_mix_g).reshape(8, 128).T),
        "g_mlp": f(norm_mlp_g).reshape(1, D),
        "g_fin": f(final_g).reshape(1, D),
        "w_in": f(w_in)[0],
        "w_pool": f(w_pool)[0],
        "pscale": np.ascontiguousarray(f(pool_scale)[0].reshape(4, 128).T),
        "invc": _inv_counts(),
        "btab": _bias_tables(f(rpb)[0]),
        "w_out": f(w_out)[0],
        "w_up": f(w_up)[0],
        "w_down": f(w_down)[0],
        "ident": np.eye(128, dtype=np.float32),
    }
    n = x.shape[0]
    in_maps = [dict(shared, x=x[b]) for b in range(n)]
    res = run_bass_kernel_spmd(nc, in_maps, core_ids=list(range(n)))
    return np.stack([np.asarray(r["out"], dtype=np.float32) for r in res.results], axis=0)
```
